# Optimizing a Trainium2 kernel written in Bass

```python
import jax, jax.numpy as jnp
from jax import lax
import numpy as np

D_MODEL = 1024
BATCH = 1
SEQ = 16384
DEPTH = 1
DEC_BATCH = 128
DEC_SEQ = 8
PAST_LEN = 16384
PAGE_SIZE = 128

ATTN_WIDTH = D_MODEL // 2
POOL_WIDTH = D_MODEL - ATTN_WIDTH
HEAD_DIM = 64
N_HEADS = ATTN_WIDTH // HEAD_DIM
N_KV_HEADS = 2
GROUP = N_HEADS // N_KV_HEADS
WINDOW = 128
BLOCK = 128
ROPE_THETA = 10000.0
POOL_WINDOWS = (2, 4, 8, 16)
N_POOL_GROUPS = len(POOL_WINDOWS)
POOL_CG = POOL_WIDTH // N_POOL_GROUPS
POOL_BUF = max(POOL_WINDOWS) - 1
D_FF = 4 * D_MODEL
PLE_DIM = 256
EPS = 1e-6
Q_COLS = N_HEADS * HEAD_DIM
KV_COLS = N_KV_HEADS * HEAD_DIM
IN_COLS = Q_COLS + 2 * KV_COLS + POOL_WIDTH
NEG = -1e30

kernel_name = "hymba_pool_swa_sink_decoder_step"


def rmsnorm(x, g):
    xf = x.astype(jnp.float32)
    y = xf * lax.rsqrt(jnp.mean(xf * xf, axis=-1, keepdims=True) + EPS)
    return (y * g.astype(jnp.float32)).astype(x.dtype)


def rope(x, pos):
    half = HEAD_DIM // 2
    inv = ROPE_THETA ** (-jnp.arange(half, dtype=jnp.float32) / half)
    ang = pos.astype(jnp.float32)[:, None] * inv[None, :]
    cos = jnp.cos(ang)[:, None, :]
    sin = jnp.sin(ang)[:, None, :]
    xf = x.astype(jnp.float32)
    x1, x2 = xf[..., :half], xf[..., half:]
    return jnp.concatenate([x1 * cos - x2 * sin, x2 * cos + x1 * sin], axis=-1).astype(x.dtype)


def window_mask(qpos, kpos):
    diff = qpos[..., :, None] - kpos[..., None, :]
    return (diff >= 0) & (diff < WINDOW) & (kpos[..., None, :] >= 0)


def sink_attention(q, k, v, mask, sinks):
    s = jnp.einsum('...qhgd,...shd->...hgqs', q.astype(jnp.float32), k.astype(jnp.float32)) * (HEAD_DIM ** -0.5)
    s = jnp.where(mask, s, NEG)
    sink = sinks.astype(jnp.float32).reshape(N_KV_HEADS, GROUP, 1, 1)
    m = jnp.maximum(jnp.max(s, axis=-1, keepdims=True), sink)
    e = jnp.exp(s - m)
    pr = e / (jnp.sum(e, axis=-1, keepdims=True) + jnp.exp(sink - m))
    return jnp.einsum('...hgqs,...shd->...qhgd', pr.astype(v.dtype), v)


def prompt_attention(q, k, v, sinks):
    B, T = q.shape[0], q.shape[1]
    nb = T // BLOCK
    qb = q.reshape(B, nb, BLOCK, N_KV_HEADS, GROUP, HEAD_DIM)
    kb = k.reshape(B, nb, BLOCK, N_KV_HEADS, HEAD_DIM)
    vb = v.reshape(B, nb, BLOCK, N_KV_HEADS, HEAD_DIM)
    pad = ((0, 0), (1, 0), (0, 0), (0, 0), (0, 0))
    kk = jnp.concatenate([jnp.pad(kb[:, :-1], pad), kb], axis=2)
    vv = jnp.concatenate([jnp.pad(vb[:, :-1], pad), vb], axis=2)
    blk = jnp.arange(nb, dtype=jnp.int32)[:, None]
    qpos = blk * BLOCK + jnp.arange(BLOCK, dtype=jnp.int32)[None, :]
    kpos = (blk - 1) * BLOCK + jnp.arange(2 * BLOCK, dtype=jnp.int32)[None, :]
    mask = window_mask(qpos, kpos)[:, None, None]
    out = sink_attention(qb, kk, vv, mask, sinks)
    return out.reshape(B, T, Q_COLS)


def sample_attention(q, k_all, v_all, pos, sinks):
    Bd, T = q.shape[0], q.shape[1]
    qg = q.reshape(Bd, T, N_KV_HEADS, GROUP, HEAD_DIM)
    kpos = pos[0] - WINDOW + jnp.arange(WINDOW + T, dtype=jnp.int32)
    mask = window_mask(pos, kpos)
    out = sink_attention(qg, k_all, v_all, mask, sinks)
    return out.reshape(Bd, T, Q_COLS)


def pool_mix(u_ext, pos, pool_w, pool_scale):
    T = pos.shape[0]
    P = POOL_BUF
    uf = u_ext.astype(jnp.float32)
    cs = jnp.concatenate([jnp.zeros_like(uf[:, :1]), jnp.cumsum(uf, axis=1)], axis=1)
    tok = uf[:, P:]
    outs = []
    for g, w in enumerate(POOL_WINDOWS):
        sl = slice(g * POOL_CG, (g + 1) * POOL_CG)
        win = cs[:, P + 1:, sl] - cs[:, P + 1 - w:P + 1 - w + T, sl]
        cnt = jnp.minimum(pos + 1, w).astype(jnp.float32)[None, :, None]
        r = win / cnt - tok[..., sl]
        outs.append(jnp.einsum('btc,cd->btd', r, pool_w[g].astype(jnp.float32)))
    return (jnp.concatenate(outs, axis=-1) * pool_scale.astype(jnp.float32)).astype(u_ext.dtype)


def trunk_layer(x, pe, pos, k_buf, v_buf, u_buf, w_in, sinks, pool_w, pool_scale, g_attn, g_pool, w_out,
                g_pre_mix, g_post_mix, g_pre_mlp, g_post_mlp, w_up, w_down, w_ple, w_ple_gate, b_ple_gate):
    B, T, _ = x.shape
    h = rmsnorm(x, g_pre_mix)
    proj = h @ w_in
    q = proj[..., :Q_COLS].reshape(B, T, N_HEADS, HEAD_DIM)
    k = proj[..., Q_COLS:Q_COLS + KV_COLS].reshape(B, T, N_KV_HEADS, HEAD_DIM)
    v = proj[..., Q_COLS + KV_COLS:Q_COLS + 2 * KV_COLS].reshape(B, T, N_KV_HEADS, HEAD_DIM)
    u = proj[..., Q_COLS + 2 * KV_COLS:]
    q = rope(q, pos)
    k = rope(k, pos)
    if k_buf is None:
        attn = prompt_attention(q, k, v, sinks)
        k_all, v_all = k, v
        u_ext = jnp.concatenate([jnp.zeros((B, POOL_BUF, POOL_WIDTH), u.dtype), u], axis=1)
    else:
        k_all = jnp.concatenate([k_buf.astype(k.dtype), k], axis=1)
        v_all = jnp.concatenate([v_buf.astype(v.dtype), v], axis=1)
        attn = sample_attention(q, k_all, v_all, pos, sinks)
        u_ext = jnp.concatenate([u_buf.astype(u.dtype), u], axis=1)
    pooled = pool_mix(u_ext, pos, pool_w, pool_scale)
    mix = jnp.concatenate([rmsnorm(attn, g_attn), rmsnorm(pooled, g_pool)], axis=-1) @ w_out
    x = x + rmsnorm(mix, g_post_mix)
    hf = rmsnorm(x, g_pre_mlp)
    ff = jnp.square(jax.nn.relu(hf @ w_up)) @ w_down
    x = x + rmsnorm(ff, g_post_mlp)
    gate = jax.nn.sigmoid(x @ w_ple_gate + b_ple_gate)
    x = x + gate * (pe @ w_ple)
    return x, k_all[:, -WINDOW:], v_all[:, -WINDOW:], u_ext[:, -POOL_BUF:]


def setup_inputs(seed: int = 0) -> dict:
    key = jax.random.key(seed)
    ks = jax.random.split(key, 24)
    f32 = jnp.float32
    nrm = lambda k, s: jax.random.normal(k, s, f32)
    gain = lambda k, n: 1.0 + 0.1 * nrm(k, (DEPTH, n))
    return {
        "x_prompt": nrm(ks[0], (BATCH, SEQ, D_MODEL)),
        "x_sample": nrm(ks[1], (DEC_BATCH, DEC_SEQ, D_MODEL)),
        "state_k": nrm(ks[2], (DEPTH, DEC_BATCH, WINDOW, N_KV_HEADS, HEAD_DIM)),
        "state_v": nrm(ks[3], (DEPTH, DEC_BATCH, WINDOW, N_KV_HEADS, HEAD_DIM)),
        "state_pool": nrm(ks[4], (DEPTH, DEC_BATCH, POOL_BUF, POOL_WIDTH)),
        "p_prompt": nrm(ks[5], (DEPTH, BATCH, SEQ, PLE_DIM)),
        "p_sample": nrm(ks[6], (DEPTH, DEC_BATCH, DEC_SEQ, PLE_DIM)),
        "w_in": nrm(ks[7], (DEPTH, D_MODEL, IN_COLS)) * D_MODEL ** -0.5,
        "attn_sinks": 0.5 * nrm(ks[8], (DEPTH, N_HEADS)),
        "pool_w": nrm(ks[9], (DEPTH, N_POOL_GROUPS, POOL_CG, POOL_CG)) * POOL_CG ** -0.5,
        "pool_scale": gain(ks[10], POOL_WIDTH),
        "g_attn_out": gain(ks[11], ATTN_WIDTH),
        "g_pool_out": gain(ks[12], POOL_WIDTH),
        "w_out": nrm(ks[13], (DEPTH, ATTN_WIDTH + POOL_WIDTH, D_MODEL)) * (ATTN_WIDTH + POOL_WIDTH) ** -0.5,
        "g_pre_mix": gain(ks[14], D_MODEL),
        "g_post_mix": gain(ks[15], D_MODEL),
        "g_pre_mlp": gain(ks[16], D_MODEL),
        "g_post_mlp": gain(ks[17], D_MODEL),
        "w_up": nrm(ks[18], (DEPTH, D_MODEL, D_FF)) * D_MODEL ** -0.5,
        "w_down": nrm(ks[19], (DEPTH, D_FF, D_MODEL)) * D_FF ** -0.5,
        "w_ple": nrm(ks[20], (DEPTH, PLE_DIM, D_MODEL)) * PLE_DIM ** -0.5,
        "w_ple_gate": nrm(ks[21], (DEPTH, D_MODEL, D_MODEL)) * D_MODEL ** -0.5,
        "b_ple_gate": 0.01 * nrm(ks[22], (DEPTH, D_MODEL)),
    }


def reference(x_prompt, x_sample, state_k, state_v, state_pool, p_prompt, p_sample, w_in, attn_sinks, pool_w,
              pool_scale, g_attn_out, g_pool_out, w_out, g_pre_mix, g_post_mix, g_pre_mlp, g_post_mlp, w_up, w_down,
              w_ple, w_ple_gate, b_ple_gate):
    pos_prompt = jnp.arange(x_prompt.shape[1], dtype=jnp.int32)
    pos_sample = PAST_LEN + jnp.arange(x_sample.shape[1], dtype=jnp.int32)
    xp, xs = x_prompt, x_sample
    kp_l, vp_l, up_l, ks_l, vs_l, us_l = [], [], [], [], [], []
    for i in range(DEPTH):
        w = (w_in[i], attn_sinks[i], pool_w[i], pool_scale[i], g_attn_out[i], g_pool_out[i], w_out[i],
             g_pre_mix[i], g_post_mix[i], g_pre_mlp[i], g_post_mlp[i], w_up[i], w_down[i], w_ple[i],
             w_ple_gate[i], b_ple_gate[i])
        xp, kp, vp, up = trunk_layer(xp, p_prompt[i], pos_prompt, None, None, None, *w)
        xs, kn, vn, un = trunk_layer(xs, p_sample[i], pos_sample, state_k[i], state_v[i], state_pool[i], *w)
        kp_l.append(kp); vp_l.append(vp); up_l.append(up)
        ks_l.append(kn); vs_l.append(vn); us_l.append(un)
    new_k_prompt = jnp.stack(kp_l)
    new_v_prompt = jnp.stack(vp_l)
    new_pool_prompt = jnp.stack(up_l)
    new_k_sample = jnp.stack(ks_l)
    new_v_sample = jnp.stack(vs_l)
    new_pool_sample = jnp.stack(us_l)
    return (xp, xs, new_k_prompt, new_v_prompt, new_pool_prompt, new_k_sample, new_v_sample, new_pool_sample)
```

```python
import math
import numpy as np
import concourse.bass as bass
import concourse.mybir as mybir
from concourse.bass_utils import run_bass_kernel_spmd

F32 = mybir.dt.float32
BF16 = mybir.dt.bfloat16
I32 = mybir.dt.int32
AF = mybir.ActivationFunctionType
ALU = mybir.AluOpType

NCORES = 8
D = 1024
NTP = 16
NT = NTP + 1
EPS = 1e-6
WIN_COLS = 1408
PI = math.pi


class Sched:
    def __init__(self):
        self.ops = []
        self.last_w = {}
        self.readers = {}

    def add(self, eng, fn, r=(), w=(), dma=None):
        r, w = list(r), list(w)
        for x in list(r):
            if isinstance(x, tuple) and x[0] == "ps":
                r.remove(x)
                if x not in w:
                    w.append(x)
        idx = len(self.ops)
        deps = set()
        for x in list(r) + list(w):
            if x in self.last_w:
                deps.add(self.last_w[x])
        for x in w:
            for rd in self.readers.get(x, ()):
                deps.add(rd)
        deps.discard(idx)
        self.ops.append(dict(eng=eng, fn=fn, deps=deps, dma=dma, sig=(dma is not None)))
        for x in w:
            self.last_w[x] = idx
            self.readers[x] = []
        for x in r:
            self.readers.setdefault(x, []).append(idx)
        return idx

    def mark(self, name):
        self.marks = getattr(self, "marks", {})
        self.marks[name] = len(self.ops)

    def barrier(self, engines):
        alldeps = set(range(len(self.ops)))
        for e in engines:
            self.ops.append(dict(eng=e, fn=None, deps=set(alldeps), dma=None, sig=False))
        self.last_w = {}
        self.readers = {}

    def finalize(self):
        ops = self.ops
        for i, op in enumerate(ops):
            for d in op["deps"]:
                if ops[d]["eng"] == "pe" and op["eng"] == "pe" and ops[d]["dma"] is None:
                    continue
                ops[d]["sig"] = True
        cnt = {}
        for op in ops:
            if op["fn"] is None:
                op["sig"] = False
            if op["sig"]:
                key = op["dma"] if op["dma"] is not None else op["eng"]
                inc = 16 if op["dma"] is not None else 1
                cnt[key] = cnt.get(key, 0) + inc
                op["semkey"] = key
                op["sigval"] = cnt[key]
        waited = {}
        for op in ops:
            e = op["eng"]
            need = {}
            for d in op["deps"]:
                dop = ops[d]
                if not dop["sig"]:
                    continue
                if dop["eng"] == "pe" and e == "pe" and dop["dma"] is None:
                    continue
                k = dop["semkey"]
                need[k] = max(need.get(k, 0), dop["sigval"])
            wl = []
            for k, v in need.items():
                if waited.get((e, k), 0) < v:
                    waited[(e, k)] = v
                    wl.append((k, v))
            op["waits"] = wl
        return sorted(cnt.keys())


def build_nc():
    nc = bass.Bass("TRN2", target_bir_lowering=False)
    S = Sched()

    def din(name, shape):
        return nc.dram_tensor(name, list(shape), F32, kind="ExternalInput").ap()

    def dout(name, shape):
        return nc.dram_tensor(name, list(shape), F32, kind="ExternalOutput").ap()

    xin = din("xin", [NT + 1, 128, D])
    pein = din("pein", [NT, 128, 256])
    pos = din("pos", [1, (NT + 1) * 128])
    invf = din("invf", [128, 1])
    w_in = din("w_in", [D, WIN_COLS])
    w_out = din("w_out", [D, D])
    w_up = din("w_up", [D, 4096])
    w_down = din("w_down", [4096, D])
    w_ple = din("w_ple", [256, D])
    w_gate = din("w_gate", [D, D])
    b_gate = din("b_gate", [1, D])
    pool_w = din("pool_w", [128, 4, 128])
    pool_scale = din("pool_scale", [1, 512])
    sinks = din("sinks", [1, 8])
    gcols = din("gcols", [128, 24])
    g_post_mix = din("g_post_mix", [1, D])
    g_post_mlp = din("g_post_mlp", [1, D])
    masks = din("masks", [128, 4, 128])
    smask = din("smask", [128, 18, 128])
    invcnt = din("invcnt", [1, 64])
    sk_ab = din("sk_ab", [128, 16, 256])
    sv = din("sv", [128, 16, 128])
    sk_raw = din("sk_raw", [16, 128, 128])
    sv_raw = din("sv_raw", [16, 128, 128])
    spool = din("spool", [16, 15, 512])

    y = dout("y", [NT, 128, D])
    kout_p = dout("kout_p", [128, 128])
    vout_p = dout("vout_p", [128, 128])
    uout_p = dout("uout_p", [15, 512])
    kout_s = dout("kout_s", [16, 128, 128])
    vout_s = dout("vout_s", [16, 128, 128])
    uout_s = dout("uout_s", [16, 15, 512])

    scr_up = nc.dram_tensor("scr_up", [8, 128, 4096], BF16, kind="Internal").ap()
    scr_dn = nc.dram_tensor("scr_dn", [8, 128, 4096], BF16, kind="Internal").ap()
    ctx = []

    def sb(name, shape, dt=F32):
        cm = nc.sbuf_tensor(name, list(shape), dt)
        t = cm.__enter__()
        ctx.append(cm)
        return t

    psc = nc.psum_tensor("ps", [128, 8, 512], F32)
    ps = psc.__enter__()
    xall = sb("xall", [128, NT, D])
    ident_b = sb("ident_b", [128, 128], BF16)
    ident_f = sb("ident_f", [128, 128])
    gcol = sb("gcol", [128, 24])
    stat = sb("stat", [128, 64])
    neghalf = sb("neghalf", [128, 8])
    junk = sb("junk", [128, D], BF16)
    xn = sb("xn", [128, D], BF16)
    gtab = sb("gtab", [128, D])
    tmpA = sb("tmpA", [128, D])
    ones_row = sb("ones_row", [1, 128], BF16)

    def PE(fn, r=(), w=()):
        return S.add("pe", fn, r, w)

    def ACT(fn, r=(), w=()):
        return S.add("act", fn, r, w)

    def DVE(fn, r=(), w=()):
        return S.add("dve", fn, r, w)

    def POOL(fn, r=(), w=()):
        return S.add("pool", fn, r, w)

    misc_i = [0]

    def DMA_SP(out, in_, sem, r=(), w=()):
        if sem == "d_misc":
            sem = "d_m%d" % misc_i[0]
            misc_i[0] += 1
        return S.add("sp", lambda e: e.dma_start(out=out, in_=in_), r, w, dma=sem)

    def DMA_POOL(out, in_, sem, r=(), w=()):
        return S.add("pool", lambda e: e.dma_start(out=out, in_=in_), r, w, dma=sem)

    stat_i = [0]

    def stat_slot(n=1):
        i = stat_i[0]
        if i + n > 64:
            i = 0
        stat_i[0] = i + n
        return i

    def rstd_op(ss_ap, n, dim, out_ap, rres, wres):
        POOL(lambda e: e.tensor_scalar(out=out_ap, in0=ss_ap, scalar1=1.0 / dim, scalar2=EPS,
                                       op0=ALU.mult, op1=ALU.add), r=rres, w=wres)
        POOL(lambda e: e.tensor_tensor(out=out_ap, in0=out_ap, in1=neghalf[:, 0:n], op=ALU.pow),
             r=list(wres) + ["neghalf"], w=wres)

    def transposes8(src_bf, src_res, bank, dst_ap, dst_res, gain_ap=None):
        bt = ps[:, bank, :].bitcast(BF16)

        def f(e):
            last = None
            for k in range(8):
                last = e.transpose(out=bt[:, k * 128:(k + 1) * 128], in_=src_bf[:, k * 128:(k + 1) * 128],
                                   identity=ident_b[:])
            return last
        PE(f, r=[src_res, "ident_b"], w=[("ps", bank)])
        btv = bt.rearrange("p (k t) -> p k t", k=8)
        if gain_ap is not None:
            DVE(lambda e: e.tensor_tensor(out=dst_ap, in0=btv, in1=gain_ap.broadcast_to([128, 8, 128]), op=ALU.mult),
                r=[("ps", bank), "gcol"], w=[dst_res])
        else:
            ACT(lambda e: e.activation(out=dst_ap, in_=btv, func=AF.Copy), r=[("ps", bank)], w=[dst_res])

    DMA_SP(gcol[:], gcols[:, :], "d_misc", w=["gcol"])
    POOL(lambda e: e.memset(neghalf[:], -0.5), w=["neghalf"])
    POOL(lambda e: e.memset(ones_row[:], 1.0), w=["ones_row"])
    POOL(lambda e: e.memset(ident_f[:], 0.0), w=["ident_f"])
    POOL(lambda e: e.affine_select(out=ident_f[:], in_=ident_f[:], pattern=[[-1, 128]], compare_op=ALU.not_equal,
                                   fill=1.0, base=0, channel_multiplier=1), r=["ident_f"], w=["ident_f"])
    POOL(lambda e: e.tensor_copy(out=ident_b[:], in_=ident_f[:]), r=["ident_f"], w=["ident_b"])


    S.mark("setup0")
    ctxA_start = len(ctx)
    wi = sb("wi", [128, 8, WIN_COLS], BF16)
    wo = sb("wo", [128, 8, D], BF16)
    pw = sb("pw", [128, 4, 128], BF16)
    hT = sb("hT", [128, 8, 640], BF16)
    qT = sb("qT", [128, 4, 512], BF16)
    kT = sb("kT", [128, 2, 640], BF16)
    kf = sb("kf", [128, 2, 128])
    vE = sb("vE", [128, 5, 2, 65], BF16)
    vf = sb("vf", [128, 128])
    ubuf = sb("ubuf", [128, 4, 528])
    sA = sb("sA", [128, 528])
    sB = sb("sB", [128, 528])
    rT = sb("rT", [128, 4, 512], BF16)
    cosT = sb("cosT", [128, 640])
    sinT = sb("sinT", [128, 640])
    rt = [sb("rt%d" % i, [128, 512]) for i in range(4)]
    PT = sb("PT", [128, 2, 4, 512], BF16)
    attn_f = sb("attn_f", [128, 512])
    cat = sb("cat", [128, D], BF16)
    catT = sb("catT", [128, 8, 128], BF16)
    maskb = sb("maskb", [128, 4, 128], BF16)
    mbias = sb("mbias", [128, 2, 512], BF16)
    mbs = sb("mbs", [128, 2, 512], BF16)
    smaskb = sb("smaskb", [128, 18, 128], BF16)
    esink = sb("esink", [128, 8])
    invcnt_b = sb("invcnt_b", [128, 4, 16])
    vsE = sb("vsE", [128, 16, 2, 65], BF16)
    us = sb("us", [128, 4, 16, 23])
    invf_t = sb("invf_t", [128, 1])
    ang = sb("ang", [128, 640])
    kq = sb("kq", [128, 640])
    kqi = kq[:].bitcast(I32)
    ksT = hT[:].rearrange("p k c -> p (k c)")[:, 0:4096].rearrange("p (a b k) -> p a b k", a=2, b=16)
    usA = sA[:, 0:368].rearrange("p (b t) -> p b t", b=16)
    usB = sB[:, 0:368].rearrange("p (b t) -> p b t", b=16)
    uo = tmpA[:, 0:512]
    pwf = attn_f[:].rearrange("p (g d) -> p g d", g=4)
    pscale_b = cat[:].bitcast(F32)

    wi_v = w_in.rearrange("(k p) n -> p k n", p=128)
    DMA_SP(invf_t[:], invf[:, :], "d_misc", w=["invf"])
    DMA_SP(gtab[:], g_post_mix.partition_broadcast(128), "d_misc", w=["gtab"])
    DMA_SP(pwf, pool_w[:, :, :], "d_misc", w=["attn_f"])
    DMA_SP(pscale_b, pool_scale.partition_broadcast(128), "d_misc", w=[("cat", 0), ("cat", 1)])
    DMA_SP(esink[:], sinks.partition_broadcast(128), "d_misc", w=["esink"])
    DMA_SP(invcnt_b[:].rearrange("p g t -> p (g t)"), invcnt.partition_broadcast(128), "d_misc", w=["invcnt_b"])
    DMA_SP(ang[:, 0:640], pos[:, 0:640].partition_broadcast(128), "d_ang", w=["ang"])
    DMA_SP(tmpA[:], xin[0, :, :], "d_x0", w=["tmpA"])
    DMA_SP(xall[:, 0, :], xin[1, :, :], "d_x1", w=[("x", 0)])
    DMA_POOL(maskb[:], masks[:, :, :], "d_maskb", w=["maskb"])
    for t in range(1, 4):
        DMA_SP(xall[:, t, :], xin[t + 1, :, :], "d_x%d" % (t + 1), w=[("x", t)])
    for si, lo in enumerate((0, 1)):
        DVE(lambda e, si=si, lo=lo: e.tensor_scalar(
            out=mbias[:, si, :].rearrange("k (p x) -> k p x", p=2),
            in0=maskb[:, lo:lo + 2, :].rearrange("k j q -> k (j q)").unsqueeze(1).broadcast_to([128, 2, 256]),
            scalar1=-1.0, scalar2=30000.0, op0=ALU.add, op1=ALU.mult), r=["maskb"], w=["mbias"])
    POOL(lambda e: e.memset(vsE[:], 1.0), w=["vsE"])

    def late_loads():
        DMA_POOL(smaskb[:], smask[:, :, :], "d_smaskb", w=["smaskb"])
        DMA_POOL(vsE[:, :, :, 0:64], sv.rearrange("p b (g d) -> p b g d", g=2), "d_vsE", w=["vsE"])
        DMA_SP(kout_s[:, 0:120, :], sk_raw[:, 8:128, :], "d_out", w=["kout_s_state"])
        DMA_SP(vout_s[:, 0:120, :], sv_raw[:, 8:128, :], "d_out", w=["vout_s_state"])
        DMA_SP(uout_s[:, 0:7, :], spool[:, 8:15, :], "d_out", w=["uout_s_state"])
    wu_v = w_up.rearrange("(k p) n -> p k n", p=128)
    wd_v = w_down.rearrange("(c p) n -> p c n", p=128)
    conv_list = []
    for b in range(8):
        conv_list.append((scr_up[b, :, :].rearrange("p (k n) -> p k n", k=8), wu_v[:, :, b * 512:(b + 1) * 512],
                          ("scr", "up", b)))
    for b in range(8):
        n_, cb = b // 4, b % 4
        conv_list.append((scr_dn[b, :, :].rearrange("p (k n) -> p k n", k=8),
                          wd_v[:, cb * 8:(cb + 1) * 8, n_ * 512:(n_ + 1) * 512], ("scr", "dn", b)))

    ALL_SCR = [c_[2] for c_ in conv_list]

    def issue_conv(n=1):
        for _ in range(n):
            if conv_list:
                o_, i_, res = conv_list.pop(0)
                DMA_POOL(o_, i_, "d_cv", w=[res])

    DVE(lambda e: e.tensor_tensor(out=pw[:], in0=pwf, in1=pscale_b.rearrange("p (g d) -> p g d", g=4),
                                  op=ALU.mult), r=["attn_f", ("cat", 0), ("cat", 1)], w=["pw"])
    ACT(lambda e: e.activation(out=esink[:], in_=esink[:], func=AF.Exp), r=["esink"], w=["esink"])
    POOL(lambda e: e.memset(vE[:], 1.0), w=["vE"])
    POOL(lambda e: e.memset(ubuf[:, :, 0:16], 0.0), w=["ubuf"])
    POOL(lambda e: e.memset(sA[:], 0.0), w=["sA"])
    POOL(lambda e: e.memset(sB[:], 0.0), w=["sB"])
    for k in range(8):
        DMA_POOL(wi[:, k, :], wi_v[:, k, :], "d_wi0" if k == 0 else "d_wi", r=["tmpA", ("x", 0)] if k == 0 else [],
                 w=[("wi", k)])
    wo_v = w_out.rearrange("(k p) n -> p k n", p=128)
    for k in range(0, 8, 4):
        DMA_POOL(wo[:, k:k + 4, :], wo_v[:, k:k + 4, :], "d_wo", w=[("wo", k)])

    C1 = 6.28125
    C2 = 2 * PI - C1

    def rope_tables(pc0, cw, tcol):
        DMA_SP(ang[:, 0:cw], pos[:, pc0:pc0 + cw].partition_broadcast(128), "d_ang", w=["ang"])
        DVE(lambda e: e.tensor_scalar(out=ang[:, 0:cw], in0=ang[:, 0:cw], scalar1=invf_t[:, 0:1], scalar2=None,
                                      op0=ALU.mult), r=["ang", "invf"], w=["ang"])
        DVE(lambda e: e.tensor_scalar(out=kq[:, 0:cw], in0=ang[:, 0:cw], scalar1=1.0 / (2 * PI), scalar2=None,
                                      op0=ALU.mult), r=["ang"], w=["kq"])
        DVE(lambda e: e.tensor_copy(out=kqi[:, 0:cw], in_=kq[:, 0:cw]), r=["kq"], w=["kq"])
        DVE(lambda e: e.tensor_copy(out=kq[:, 0:cw], in_=kqi[:, 0:cw]), r=["kq"], w=["kq"])
        DVE(lambda e: e.scalar_tensor_tensor(out=ang[:, 0:cw], in0=kq[:, 0:cw], scalar=-C1, in1=ang[:, 0:cw],
                                             op0=ALU.mult, op1=ALU.add), r=["kq", "ang"], w=["ang"])
        DVE(lambda e: e.scalar_tensor_tensor(out=ang[:, 0:cw], in0=kq[:, 0:cw], scalar=-C2, in1=ang[:, 0:cw],
                                             op0=ALU.mult, op1=ALU.add), r=["kq", "ang"], w=["ang"])
        DVE(lambda e: e.tensor_scalar(out=kq[:, 0:cw], in0=ang[:, 0:cw], scalar1=-PI, scalar2=PI, op0=ALU.max,
                                      op1=ALU.min), r=["ang"], w=["kq"])
        ACT(lambda e: e.activation(out=sinT[:, tcol:tcol + cw], in_=kq[:, 0:cw], func=AF.Sin), r=["kq"], w=["sinT"])
        DVE(lambda e: e.tensor_scalar(out=kq[:, 0:cw], in0=ang[:, 0:cw], scalar1=PI / 2, scalar2=None, op0=ALU.add),
            r=["ang"], w=["kq"])
        DVE(lambda e: e.tensor_scalar(out=ang[:, 0:cw], in0=kq[:, 0:cw], scalar1=PI, scalar2=2 * PI, op0=ALU.is_gt,
                                      op1=ALU.mult), r=["kq"], w=["ang"])
        DVE(lambda e: e.tensor_tensor(out=kq[:, 0:cw], in0=kq[:, 0:cw], in1=ang[:, 0:cw], op=ALU.subtract),
            r=["kq", "ang"], w=["kq"])
        DVE(lambda e: e.tensor_scalar(out=kq[:, 0:cw], in0=kq[:, 0:cw], scalar1=-PI, scalar2=PI, op0=ALU.max,
                                      op1=ALU.min), r=["kq"], w=["kq"])
        ACT(lambda e: e.activation(out=cosT[:, tcol:tcol + cw], in_=kq[:, 0:cw], func=AF.Sin), r=["kq"], w=["cosT"])

    def rope_tables_split(pc0, cw, tcol, sarg=None, sres="sA", dma=True):
        if sarg is None:
            sarg = sA[:, 0:cw]

        def part0():
            DMA_SP(ang[:, 0:cw], pos[:, pc0:pc0 + cw].partition_broadcast(128), "d_ang", w=["ang"])

        def part1():
            DVE(lambda e: e.tensor_scalar(out=ang[:, 0:cw], in0=ang[:, 0:cw], scalar1=invf_t[:, 0:1], scalar2=None,
                                          op0=ALU.mult), r=["ang", "invf"], w=["ang"])
            DVE(lambda e: e.tensor_scalar(out=kq[:, 0:cw], in0=ang[:, 0:cw], scalar1=1.0 / (2 * PI), scalar2=None,
                                          op0=ALU.mult), r=["ang"], w=["kq"])
            DVE(lambda e: e.tensor_copy(out=kqi[:, 0:cw], in_=kq[:, 0:cw]), r=["kq"], w=["kq"])
            DVE(lambda e: e.tensor_copy(out=kq[:, 0:cw], in_=kqi[:, 0:cw]), r=["kq"], w=["kq"])
            DVE(lambda e: e.scalar_tensor_tensor(out=ang[:, 0:cw], in0=kq[:, 0:cw], scalar=-C1, in1=ang[:, 0:cw],
                                                 op0=ALU.mult, op1=ALU.add), r=["kq", "ang"], w=["ang"])
            DVE(lambda e: e.scalar_tensor_tensor(out=ang[:, 0:cw], in0=kq[:, 0:cw], scalar=-C2, in1=ang[:, 0:cw],
                                                 op0=ALU.mult, op1=ALU.add), r=["kq", "ang"], w=["ang"])
            DVE(lambda e: e.tensor_scalar(out=sarg, in0=ang[:, 0:cw], scalar1=-PI, scalar2=PI, op0=ALU.max,
                                          op1=ALU.min), r=["ang"], w=[sres])

        def part2():
            DVE(lambda e: e.tensor_scalar(out=kq[:, 0:cw], in0=ang[:, 0:cw], scalar1=PI / 2, scalar2=None, op0=ALU.add),
                r=["ang"], w=["kq"])
            DVE(lambda e: e.tensor_scalar(out=ang[:, 0:cw], in0=kq[:, 0:cw], scalar1=PI, scalar2=2 * PI, op0=ALU.is_gt,
                                          op1=ALU.mult), r=["kq"], w=["ang"])
            DVE(lambda e: e.tensor_tensor(out=kq[:, 0:cw], in0=kq[:, 0:cw], in1=ang[:, 0:cw], op=ALU.subtract),
                r=["kq", "ang"], w=["kq"])
            DVE(lambda e: e.tensor_scalar(out=kq[:, 0:cw], in0=kq[:, 0:cw], scalar1=-PI, scalar2=PI, op0=ALU.max,
                                          op1=ALU.min), r=["kq"], w=["kq"])

        def part3():
            ACT(lambda e: e.activation(out=sinT[:, tcol:tcol + cw], in_=sarg, func=AF.Sin), r=[sres], w=["sinT"])
            ACT(lambda e: e.activation(out=cosT[:, tcol:tcol + cw], in_=kq[:, 0:cw], func=AF.Sin), r=["kq"], w=["cosT"])
        if dma:
            part0()
        return [part1, part2, part3]

    rp0 = rope_tables_split(0, 640, 0, sarg=ubuf[:, 0:2, :].rearrange("p g c -> p (g c)")[:, 0:640], sres="ubuf",
                            dma=False)
    rp0[0]()
    rp0[1]()
    for t in range(4, NT):
        DMA_SP(xall[:, t, :], xin[t + 1, :, :], "d_x%d" % (t + 1), r=[("wi", 7)], w=[("x", t)])

    def load_spool():
        for half in range(2):
            spf = rt[3][0:120, :]
            DMA_SP(spf, spool[half * 8:(half + 1) * 8, :, :].rearrange("b i c -> (b i) c"), "d_misc", w=["rt3"])
            for g in range(4):
                bank = 6 + (half * 4 + g) % 2
                PE(lambda e, g=g, bank=bank, spf=spf: e.transpose(out=ps[:, bank, 0:120], in_=spf[:, g * 128:(g + 1) * 128],
                                                                  identity=ident_f[0:120, 0:120]),
                   r=["rt3", "ident_f"], w=[("ps", bank)])
                ACT(lambda e, half=half, g=g, bank=bank: e.activation(
                    out=us[:, g, half * 8:(half + 1) * 8, 0:15],
                    in_=ps[:, bank, 0:120].rearrange("p (b i) -> p b i", b=8), func=AF.Copy),
                    r=[("ps", bank)], w=["us"])

    def state_k_load(q4):
        rtb = rt[q4 % 3][:].bitcast(BF16).rearrange("p (b c) -> p b c", b=4)
        DMA_POOL(rtb, sk_ab[:, q4 * 4:(q4 + 1) * 4, :], "d_sk%d" % q4, w=["rt%d" % (q4 % 3)])

    def build_state_kT():
        for q4 in range(4):
            rtb = rt[q4 % 3][:].bitcast(BF16).rearrange("p (b c) -> p b c", b=4)
            rres = "rt%d" % (q4 % 3)
            if q4 == 3:
                state_k_load(q4)
            for bb in range(4):
                b = q4 * 4 + bb
                for ab in range(2):
                    bank = 6 + ab
                    bt = ps[:, bank, :].bitcast(BF16)
                    PE(lambda e, bb=bb, ab=ab, bt=bt, rtb=rtb: e.transpose(
                        out=bt[:, 0:128], in_=rtb[:, bb, ab * 128:(ab + 1) * 128], identity=ident_b[:]),
                       r=[rres, "ident_b"], w=[("ps", bank)])
                    ACT(lambda e, b=b, ab=ab, bt=bt: e.activation(out=ksT[:, ab, b, :], in_=bt[:, 0:128], func=AF.Copy),
                        r=[("ps", bank)], w=["hT"])

    S.mark("setup")
    groups = [[0, 1, 2, 3, 4], [5, 6, 7, 8], [9, 10, 11, 12], [13, 14, 15, 16], [17]]
    PAIRS = [(0, 1, "q", 0), (2, 3, "q", 2), (4, 5, "k", 0)]

    def head_of(p, r):
        return (r % 2) + 4 * (r // 2) + 2 * p

    def attention(tile, qcol, kblocks, set_i, ob0, pe_mask=None, pe_mask_fn=None, hook=None):
        nb = len(kblocks)
        nr = (nb + 1) // 2
        first_o = [True, True]
        for rd in range(nr):
            blks = kblocks[rd * 2: rd * 2 + 2]
            nj = len(blks)
            if pe_mask_fn is not None:
                pe_mask = pe_mask_fn(rd)
            for r in range(4):
                sbk = r % 2

                def fqk(e, r=r, blks=blks, sbk=sbk, pe_mask=pe_mask):
                    last = None
                    first = True
                    if pe_mask is not None:
                        e.matmul(ps[:, sbk, :], lhsT=ident_b[:], rhs=pe_mask, start=True, stop=False,
                                 skip_group_check=True)
                        first = False
                    for p in range(2):
                        for j, blk in enumerate(blks):
                            o = ps[:, sbk, (p * 2 + j) * 128:(p * 2 + j + 1) * 128]
                            e.matmul(o, lhsT=blk[0](r), rhs=qT[32 * r:32 * r + 32, 2 * p, qcol:qcol + 128],
                                     start=first, stop=False, tile_position=(32 * r, 0), skip_group_check=True)
                            first = False
                            last = e.matmul(o, lhsT=blk[1](r), rhs=qT[32 * r:32 * r + 32, 2 * p + 1, qcol:qcol + 128],
                                            start=False, stop=True, tile_position=(32 * r, 0), skip_group_check=True)
                    return last
                PE(fqk, r=["qT", "mbias", ("mbs", rd % 2), "ident_b"] + [b_[4] for b_ in blks], w=[("ps", sbk)])
                pt = PT[:, set_i, r, :]
                ACT(lambda e, sbk=sbk, pt=pt: e.activation(out=pt, in_=ps[:, sbk, :], func=AF.Exp, scale=0.125),
                    r=[("ps", sbk)], w=[("PT", set_i, r)])
                if pe_mask is None:
                    if nj == 2:
                        m2 = blks[0][3]
                        POOL(lambda e, pt=pt, m2=m2: e.tensor_tensor(
                            out=pt.rearrange("k (p j q) -> k p (j q)", p=2, j=2),
                            in0=pt.rearrange("k (p j q) -> k p (j q)", p=2, j=2),
                            in1=m2.rearrange("k j q -> k (j q)").unsqueeze(1).broadcast_to([128, 2, 256]), op=ALU.mult),
                            r=[("PT", set_i, r), "maskb", "smaskb"], w=[("PT", set_i, r)])
                    else:
                        m1 = blks[0][3]
                        ptv = pt.rearrange("k (p j q) -> k p j q", p=2, j=2)[:, :, 0, :]
                        POOL(lambda e, ptv=ptv, m1=m1: e.tensor_tensor(
                            out=ptv, in0=ptv, in1=m1.unsqueeze(1).broadcast_to([128, 2, 128]), op=ALU.mult),
                            r=[("PT", set_i, r), "maskb", "smaskb"], w=[("PT", set_i, r)])
                if hook is not None:
                    hook(rd, r)
            for r in range(4):
                def fpv(e, r=r, blks=blks, rd=rd):
                    last = None
                    for p in range(2):
                        h = head_of(p, r)
                        ob = ob0 + h // 4
                        for j, blk in enumerate(blks):
                            st = first_o[h // 4] and True
                            first_o[h // 4] = False
                            last = e.matmul(ps[:, ob, (h % 4) * 65:(h % 4) * 65 + 65],
                                            lhsT=PT[:, set_i, r, (p * 2 + j) * 128:(p * 2 + j + 1) * 128],
                                            rhs=blk[2](r // 2), start=st, stop=(rd == nr - 1 and j == nj - 1),
                                            skip_group_check=True)
                    return last
                PE(fpv, r=[("PT", set_i, r)] + [b_[5] for b_ in blks], w=[("ps", ob0), ("ps", ob0 + 1)])

    def attn_normalize(ob0):
        s0 = stat_slot(8)
        den = stat[:, s0:s0 + 8]
        for ob in range(2):
            ov = ps[:, ob0 + ob, 0:260].rearrange("q (h d) -> q h d", h=4)
            DVE(lambda e, ob=ob, ov=ov: e.tensor_tensor(out=den[:, ob * 4:(ob + 1) * 4].unsqueeze(2), in0=ov[:, :, 64:65],
                                                        in1=esink[:, ob * 4:(ob + 1) * 4].unsqueeze(2), op=ALU.add),
                r=[("ps", ob0 + ob), "esink"], w=[("den", s0)])
        DVE(lambda e: e.reciprocal(out=den, in_=den), r=[("den", s0)], w=[("den", s0)])
        for ob in range(2):
            ov = ps[:, ob0 + ob, 0:260].rearrange("q (h d) -> q h d", h=4)
            DVE(lambda e, ob=ob, ov=ov: e.tensor_tensor(
                out=attn_f[:, ob * 256:(ob + 1) * 256].rearrange("q (h d) -> q h d", h=4), in0=ov[:, :, 0:64],
                in1=den[:, ob * 4:(ob + 1) * 4].unsqueeze(2).broadcast_to([128, 4, 64]), op=ALU.mult),
                r=[("ps", ob0 + ob), ("den", s0)], w=["attn_f"])

    def rope_pair(bankA, bankB, ccol, n, dstA, dstB, dres, f32dst=None):
        Cc = cosT[:, ccol:ccol + n]
        Sc = sinT[:, ccol:ccol + n]
        A = ps[:, bankA, 0:n]
        B = ps[:, bankB, 0:n]
        DVE(lambda e: e.tensor_tensor(out=rt[0][:, 0:n], in0=A, in1=Cc, op=ALU.mult), r=[("ps", bankA), "cosT"], w=["rt0"])
        DVE(lambda e: e.tensor_tensor(out=rt[1][:, 0:n], in0=B, in1=Sc, op=ALU.mult), r=[("ps", bankB), "sinT"], w=["rt1"])
        DVE(lambda e: e.tensor_tensor(out=rt[2][:, 0:n], in0=B, in1=Cc, op=ALU.mult), r=[("ps", bankB), "cosT"], w=["rt2"])
        DVE(lambda e: e.tensor_tensor(out=rt[3][:, 0:n], in0=A, in1=Sc, op=ALU.mult), r=[("ps", bankA), "sinT"], w=["rt3"])
        DVE(lambda e: e.tensor_tensor(out=dstA, in0=rt[0][:, 0:n], in1=rt[1][:, 0:n], op=ALU.subtract),
            r=["rt0", "rt1"], w=[dres])
        DVE(lambda e: e.tensor_tensor(out=dstB, in0=rt[2][:, 0:n], in1=rt[3][:, 0:n], op=ALU.add),
            r=["rt2", "rt3"], w=[dres])
        if f32dst is not None:
            off = f32dst
            POOL(lambda e: e.tensor_tensor(out=kf[:, 0, :], in0=rt[0][:, off:off + 128], in1=rt[1][:, off:off + 128],
                                           op=ALU.subtract), r=["rt0", "rt1"], w=["kf"])
            POOL(lambda e: e.tensor_tensor(out=kf[:, 1, :], in0=rt[2][:, off:off + 128], in1=rt[3][:, off:off + 128],
                                           op=ALU.add), r=["rt2", "rt3"], w=["kf"])

    WI_ALL = [("wi", k) for k in range(8)]

    def proj_chunk(bank, chunk, c0, n):
        def f(e):
            last = None
            for k in range(8):
                last = e.matmul(ps[:, bank, 0:n], lhsT=wi[:, k, chunk * 128:(chunk + 1) * 128], rhs=hT[:, k, c0:c0 + n],
                                start=(k == 0), stop=(k == 7))
            return last
        PE(f, r=WI_ALL + ["hT"], w=[("ps", bank)])

    def emit_kv_out(kdst_list, vdst_list):
        for ab in range(2):
            PE(lambda e, ab=ab: e.transpose(out=ps[:, 6 + ab, 0:128], in_=kf[:, ab, :], identity=ident_f[:]),
               r=["kf", "ident_f"], w=[("ps", 6 + ab)])
        ko = tmpA[:, 0:128].rearrange("t (g h d) -> t g h d", g=2, h=2)
        for ab in range(2):
            src = ps[:, 6 + ab, 0:128].rearrange("t (g x) -> t g x", g=2)[:, :, 0:32]
            ACT(lambda e, ab=ab, src=src: e.activation(out=ko[:, :, ab, :], in_=src, func=AF.Copy),
                r=[("ps", 6 + ab)], w=["tmpA"])
        tag = "s" if len(kdst_list) > 1 else "p"
        for (dst, src_lo, src_hi) in kdst_list:
            DMA_SP(dst, tmpA[src_lo:src_hi, 0:128], "d_ko" + tag, r=["tmpA"], w=[("o", id(dst))])
        for (dst, src_lo, src_hi) in vdst_list:
            DMA_SP(dst, vf[src_lo:src_hi, :], "d_vo" + tag, r=["vf"], w=[("o", id(dst))])

    def A1a(gi, t):
        tiles = groups[gi]
        real = [t_ for t_ in tiles if t_ != 0]
        xt = tmpA[:] if t == 0 else xall[:, t - 1, :]
        xres = "tmpA" if t == 0 else ("x", t - 1)
        blk = 0 if t == 0 else (real.index(t) + 1)
        s0 = stat_slot(1)
        ACT(lambda e, xt=xt, s0=s0: e.activation(out=junk[:], in_=xt, func=AF.Square, accum_out=stat[:, s0:s0 + 1]),
            r=[xres], w=["junk", ("st", s0)])
        rstd_op(stat[:, s0:s0 + 1], 1, D, stat[:, s0:s0 + 1], [("st", s0)], [("st", s0)])
        return (blk, xt, xres, s0)

    def A1s(h):
        blk, xt, xres, s0 = h
        DVE(lambda e, xt=xt, s0=s0: e.tensor_scalar(out=xn[:], in0=xt, scalar1=stat[:, s0:s0 + 1], scalar2=None,
                                                    op0=ALU.mult), r=[xres, ("st", s0)], w=["xn0"])

    def A1b(h):
        blk = h[0]
        transposes8(xn, "xn0", 7, hT[:, :, blk * 128:(blk + 1) * 128], "hT", gain_ap=gcol[:, 0:8].unsqueeze(2))

    def A1(gi):
        for t in groups[gi]:
            h = A1a(gi, t)
            A1s(h)
            A1b(h)

    def A3(gi, tiles, real, ntr, sample, n):
        n = ntr * 128
        if not sample:
            for g in range(4):
                w_ = 2 << g
                cur = ubuf[:, g, :]
                cres = "ubuf"
                width = 16 + n
                step = 1
                bufs = [(sA, "sA"), (sB, "sB")]
                bi = 0
                while step < w_:
                    dstt, dres_ = bufs[bi]
                    DVE(lambda e, cur=cur, dstt=dstt, step=step, width=width: e.tensor_tensor(
                        out=dstt[:, step:width], in0=cur[:, step:width], in1=cur[:, 0:width - step], op=ALU.add),
                        r=[cres], w=[dres_])
                    if step > 1:
                        pass
                    cur, cres = dstt, dres_
                    step *= 2
                    bi ^= 1
                DVE(lambda e, cur=cur, g=g, w_=w_, n=n: e.scalar_tensor_tensor(
                    out=rT[:, g, 0:n], in0=cur[:, 16:16 + n], scalar=1.0 / w_, in1=ubuf[:, g, 16:16 + n],
                    op0=ALU.mult, op1=ALU.subtract), r=[cres, "ubuf"], w=["rT"])
                if gi == 0:
                    DVE(lambda e, cur=cur, g=g: e.tensor_tensor(out=sA[:, 0:16] if cur is not sA else sB[:, 0:16],
                                                                in0=cur[:, 16:32], in1=invcnt_b[:, g, :], op=ALU.mult),
                        r=[cres, "invcnt_b"], w=["sA", "sB"])
                    DVE(lambda e, cur=cur, g=g: e.tensor_tensor(out=rT[:, g, 0:16],
                                                                in0=sA[:, 0:16] if cur is not sA else sB[:, 0:16],
                                                                in1=ubuf[:, g, 16:32], op=ALU.subtract),
                        r=["sA", "sB", "ubuf"], w=["rT"])
            if gi == 3:
                for g in range(4):
                    PE(lambda e, g=g, n=n: e.transpose(out=ps[0:15, 6, g * 128:(g + 1) * 128],
                                                       in_=ubuf[:, g, 16 + n - 15:16 + n], identity=ident_f[:]),
                       r=["ubuf", "ident_f"], w=[("ps", 6)])
                ACT(lambda e: e.activation(out=uo[0:15, :], in_=ps[0:15, 6, :], func=AF.Copy), r=[("ps", 6)], w=["tmpA"])
                DMA_SP(uout_p[:, :], uo[0:15, :], "d_uop", r=["tmpA"], w=["uout_p"])
        else:
            for g in range(4):
                POOL(lambda e, g=g: e.tensor_copy(out=us[:, g, :, 15:23],
                                                  in_=ubuf[:, g, 16:144].rearrange("c (b t) -> c b t", b=16)),
                     r=["ubuf"], w=["us"])
                w_ = 2 << g
                cur = us[:, g, :, :]
                cres = "us"
                step = 1
                bufs = [(usA, "sA"), (usB, "sB")]
                bi = 0
                while step < w_:
                    dstt, dres_ = bufs[bi]
                    POOL(lambda e, cur=cur, dstt=dstt, step=step: e.tensor_tensor(
                        out=dstt[:, :, step:23], in0=cur[:, :, step:23], in1=cur[:, :, 0:23 - step], op=ALU.add),
                        r=[cres], w=[dres_])
                    cur, cres = dstt, dres_
                    step *= 2
                    bi ^= 1
                DVE(lambda e, cur=cur, g=g, w_=w_: e.scalar_tensor_tensor(
                    out=rT[:, g, 0:128].rearrange("c (b t) -> c b t", b=16), in0=cur[:, :, 15:23], scalar=1.0 / w_,
                    in1=us[:, g, :, 15:23], op0=ALU.mult, op1=ALU.subtract), r=[cres, "us"], w=["rT"])
            for g in range(4):
                PE(lambda e, g=g: e.transpose(out=ps[:, 6, g * 128:(g + 1) * 128], in_=ubuf[:, g, 16:144],
                                              identity=ident_f[:]), r=["ubuf", "ident_f"], w=[("ps", 6)])
            ACT(lambda e: e.activation(out=uo[:], in_=ps[:, 6, :], func=AF.Copy), r=[("ps", 6)], w=["tmpA"])
            for b in range(16):
                DMA_SP(uout_s[b, 7:15, :], uo[b * 8:(b + 1) * 8, :], "d_uos", r=["tmpA"], w=[("uout_s", b)])

    A1(0)
    rp0[2]()
    for gi, tiles in enumerate(groups):
        sample = (gi == 4)
        real = [t for t in tiles if t != 0]
        ntr = len(real)
        S.mark("g%d_A1" % gi)
        c_lo = 0 if gi == 0 else 128
        ncols = (len(tiles)) * 128 if gi == 0 else ntr * 128
        tabcol = tiles[0] * 128
        if gi == 0:
            kv_segs = [(0, 128, 0), (128, 512, 128)]
        else:
            kv_segs = [(128, ntr * 128, 128)]
        pair_i = 0
        for (ca, cb_, dest, di) in PAIRS:
            if dest == "q":
                c0, n, tc0 = 128, ntr * 128, 128
            else:
                c0, n, tc0 = c_lo, ncols, tabcol
            segs = [(c0, n, tc0)] if dest == "q" else kv_segs
            for (sc0, sn, stc) in segs:
                bA = 2 * (pair_i % 2)
                pair_i += 1
                proj_chunk(bA, ca, sc0, sn)
                proj_chunk(bA + 1, cb_, sc0, sn)
                if dest == "q":
                    rope_pair(bA, bA + 1, stc, sn, qT[:, di, 0:sn], qT[:, di + 1, 0:sn], "qT")
                else:
                    f32off = None
                    if (gi == 3 and sc0 == 128) or sample:
                        f32off = (sn - 128)
                    rope_pair(bA, bA + 1, stc, sn, kT[:, 0, sc0:sc0 + sn], kT[:, 1, sc0:sc0 + sn], "kT", f32dst=f32off)
        if sample:
            for q4 in range(3):
                state_k_load(q4)
        for g in range(4):
            for (sc0, sn, _stc) in kv_segs:
                ub = 4 + g
                proj_chunk(ub, 6 + g, sc0, sn)
                if sc0 < 128:
                    ACT(lambda e, g=g, ub=ub: e.activation(out=ubuf[:, g, 0:16], in_=ps[:, ub, 112:128],
                                                           func=AF.Copy), r=[("ps", ub)], w=["ubuf"])
                else:
                    ucol = 16 + sc0 - 128
                    ACT(lambda e, g=g, sn=sn, ucol=ucol, ub=ub: e.activation(out=ubuf[:, g, ucol:ucol + sn],
                                                                             in_=ps[:, ub, 0:sn], func=AF.Copy),
                        r=[("ps", ub)], w=["ubuf"])
        A3(gi, tiles, real, ntr, sample, ntr * 128)
        for vi, t in enumerate(tiles):
            blk = 0 if t == 0 else (real.index(t) + 1)
            vb = vi % 2

            def fv(e, blk=blk, vb=vb):
                last = None
                for k in range(8):
                    last = e.matmul(ps[:, vb, 0:128], lhsT=hT[:, k, blk * 128:(blk + 1) * 128], rhs=wi[:, k, 1280:1408],
                                    start=(k == 0), stop=(k == 7))
                return last
            PE(fv, r=WI_ALL + ["hT"], w=[("ps", vb)])
            ACT(lambda e, blk=blk, vb=vb: e.activation(out=vE[:, blk, :, 0:64],
                                                       in_=ps[:, vb, 0:128].rearrange("t (g d) -> t g d", g=2),
                                                       func=AF.Copy),
                r=[("ps", vb)], w=["vE"])
            if t == 16 or t == 17:
                DVE(lambda e, vb=vb: e.tensor_copy(out=vf[:], in_=ps[:, vb, 0:128]), r=[("ps", vb)], w=["vf"])
        if gi == 3:
            emit_kv_out([(kout_p[:, :], 0, 128)], [(vout_p[:, :], 0, 128)])
        if sample:
            emit_kv_out([(kout_s[b, 120:128, :], b * 8, b * 8 + 8) for b in range(16)],
                        [(vout_s[b, 120:128, :], b * 8, b * 8 + 8) for b in range(16)])
        S.mark("g%d_A2" % gi)
        n = ntr * 128

        def S1(ti, t, hook=None):
            qcol = ti * 128
            ob0 = 2 + 2 * (ti % 2)
            issue_conv(1)
            if not sample:
                kb = []
                for j in range(2):
                    kc = (ti + j) * 128
                    blkv = ti + j
                    kb.append((lambda r, kc=kc: kT[32 * r:32 * r + 32, 0, kc:kc + 128],
                               lambda r, kc=kc: kT[32 * r:32 * r + 32, 1, kc:kc + 128],
                               lambda g, blkv=blkv: vE[:, blkv, g, :],
                               None, "kT", "vE"))
                if t == 1:
                    kb = [kb[1], kb[0]]
                    attention(t, qcol, kb, ti % 2, ob0, pe_mask=mbias[:, 1, :], hook=hook)
                else:
                    attention(t, qcol, kb, ti % 2, ob0, pe_mask=mbias[:, 0, :], hook=hook)
            else:
                build_state_kT()
                kb = []
                for b in range(16):
                    kb.append((lambda r, b=b: ksT[32 * r:32 * r + 32, 0, b, :],
                               lambda r, b=b: ksT[32 * r:32 * r + 32, 1, b, :],
                               lambda g, b=b: vsE[:, b, g, :],
                               smaskb[:, b:b + 2, :] if b % 2 == 0 else None, "hT", "vsE"))
                kb.append((lambda r: kT[32 * r:32 * r + 32, 0, 128:256],
                           lambda r: kT[32 * r:32 * r + 32, 1, 128:256],
                           lambda g: vE[:, 1, g, :], smaskb[:, 16, :], "kT", "vE"))
                def smask_bias(rd):
                    slot = rd % 2
                    DVE(lambda e, rd=rd, slot=slot: e.tensor_scalar(
                        out=mbs[:, slot, :].rearrange("k (p x) -> k p x", p=2),
                        in0=smaskb[:, 2 * rd:2 * rd + 2, :].rearrange("k j q -> k (j q)").unsqueeze(1).broadcast_to(
                            [128, 2, 256]),
                        scalar1=-1.0, scalar2=30000.0, op0=ALU.add, op1=ALU.mult),
                        r=["smaskb"], w=[("mbs", slot)])
                    return mbs[:, slot, :]
                attention(t, qcol, kb, 0, ob0, pe_mask=None, pe_mask_fn=smask_bias)

        s2slot = {}

        def S2norm(ti):
            attn_normalize(2 + 2 * (ti % 2))

        def S2pool(ti):
            qcol = ti * 128

            def fpool(e, qcol=qcol):
                last = None
                for g in range(4):
                    last = e.matmul(ps[:, 6, g * 128:(g + 1) * 128], lhsT=rT[:, g, qcol:qcol + 128], rhs=pw[:, g, :],
                                    start=(g == 0), stop=(g == 3), skip_group_check=True)
                return last
            PE(fpool, r=["rT", "pw"], w=[("ps", 6)])

        def S2statsA(ti):
            s0 = stat_slot(2)
            s2slot[ti] = s0
            ACT(lambda e, s0=s0: e.activation(out=junk[:, 0:512], in_=attn_f[:], func=AF.Square,
                                              accum_out=stat[:, s0:s0 + 1]), r=["attn_f"], w=["junk", ("st", s0)])

        def S2statsB(ti):
            s0 = s2slot[ti]
            ACT(lambda e, s0=s0: e.activation(out=junk[:, 512:1024], in_=ps[:, 6, :], func=AF.Square,
                                              accum_out=stat[:, s0 + 1:s0 + 2]), r=[("ps", 6)], w=["junk", ("st", s0)])
            rstd_op(stat[:, s0:s0 + 2], 2, 512, stat[:, s0:s0 + 2], [("st", s0)], [("st", s0)])

        def S2cat(ti):
            s0 = s2slot[ti]
            ACT(lambda e, s0=s0: e.activation(out=cat[:, 0:512], in_=attn_f[:], func=AF.Copy, scale=stat[:, s0:s0 + 1]),
                r=["attn_f", ("st", s0)], w=[("cat", 0)])
            ACT(lambda e, s0=s0: e.activation(out=cat[:, 512:1024], in_=ps[:, 6, :], func=AF.Copy,
                                              scale=stat[:, s0 + 1:s0 + 2]), r=[("ps", 6), ("st", s0)], w=[("cat", 1)])

        def S2b(ti, t):
            for h in range(2):
                bank = 7 - h
                bt = ps[:, bank, :].bitcast(BF16)

                def f(e, h=h, bt=bt):
                    last = None
                    for k in range(4):
                        kk = h * 4 + k
                        last = e.transpose(out=bt[:, k * 128:(k + 1) * 128], in_=cat[:, kk * 128:(kk + 1) * 128],
                                           identity=ident_b[:])
                    return last
                PE(f, r=[("cat", h), "ident_b"], w=[("ps", bank)])
            for h in range(2):
                bank = 7 - h
                btv = ps[:, bank, :].bitcast(BF16)[:, 0:512].rearrange("p (k t) -> p k t", k=4)
                DVE(lambda e, h=h, btv=btv: e.tensor_tensor(
                    out=catT[:, h * 4:(h + 1) * 4, :], in0=btv,
                    in1=gcol[:, 16 + h * 4:20 + h * 4].unsqueeze(2).broadcast_to([128, 4, 128]), op=ALU.mult),
                    r=[("ps", bank), "gcol"], w=[("catT", h)])

        def S3(ti, t):
            ob0 = 2 + 2 * (ti % 2)

            for h in range(2):
                def fmix(e, h=h):
                    last = None
                    for n_ in range(2):
                        for k in range(h * 4, h * 4 + 4):
                            last = e.matmul(ps[:, ob0 + n_, :], lhsT=catT[:, k, :], rhs=wo[:, k, n_ * 512:(n_ + 1) * 512],
                                            start=(k == 0), stop=(k == 7), skip_group_check=True)
                    return last
                PE(fmix, r=[("catT", h), ("wo", 0), ("wo", 4)], w=[("ps", ob0), ("ps", ob0 + 1)])
            s1 = stat_slot(1)
            mixv = ps[:, ob0:ob0 + 2, :]
            ACT(lambda e, s1=s1: e.activation(out=junk[:].rearrange("p (a b) -> p a b", a=2), in_=mixv, func=AF.Square,
                                              accum_out=stat[:, s1:s1 + 1]),
                r=[("ps", ob0), ("ps", ob0 + 1)], w=["junk", ("st", s1)])
            rstd_op(stat[:, s1:s1 + 1], 1, D, stat[:, s1:s1 + 1], [("st", s1)], [("st", s1)])
            for n_ in range(2):
                DVE(lambda e, s1=s1, n_=n_: e.scalar_tensor_tensor(
                    out=tmpA[:, n_ * 512:(n_ + 1) * 512], in0=ps[:, ob0 + n_, :], scalar=stat[:, s1:s1 + 1],
                    in1=gtab[:, n_ * 512:(n_ + 1) * 512], op0=ALU.mult, op1=ALU.mult),
                    r=[("ps", ob0 + n_), ("st", s1), "gtab"], w=["tmpA"])

        def S3add(ti, t):
            POOL(lambda e, t=t: e.tensor_tensor(out=xall[:, t - 1, :], in0=xall[:, t - 1, :], in1=tmpA[:], op=ALU.add),
                 r=["tmpA", ("x", t - 1)], w=[("x", t - 1)])

        S1(0, real[0])
        S2norm(0)
        nxt = groups[gi + 1] if gi + 1 < len(groups) else []
        pend = [None]
        rparts = rope_tables_split(nxt[0] * 128, len(nxt) * 128, 128) if nxt else []
        for ti, t in enumerate(real):
            S2pool(ti)
            S2statsA(ti)

            def mid(ti=ti):
                S2statsB(ti)
                if pend[0] is not None:
                    S3add(*pend[0])
                    pend[0] = None

            if ti + 1 < ntr:
                def hook(rd, r, ti=ti, mid=mid):
                    if rd == 0 and r == 0:
                        mid()
                    if rd == 0 and r == 3:
                        S2cat(ti)
                S1(ti + 1, real[ti + 1], hook=hook)
            else:
                mid()
                S2cat(ti)
            nh = A1a(gi + 1, nxt[ti]) if ti < len(nxt) else None
            S2b(ti, t)
            if nh is not None:
                A1s(nh)
            if ti + 1 < ntr:
                S2norm(ti + 1)
            if rparts and ti < 2:
                rparts[ti]()
            if nh is not None and ti + 1 == ntr:
                A1b(nh)
                nh = None
            S3(ti, t)
            pend[0] = (ti, t)
            if nh is not None:
                A1b(nh)
            if rparts and ti == 2:
                rparts[2]()
            if ti == 0 and gi == 0:
                load_spool()
            if ti == 0 and gi == 1:
                late_loads()
        if pend[0] is not None:
            S3add(*pend[0])
        S.mark("g%d_A4" % gi)
        if gi < 3:
            lastb = ntr
            POOL(lambda e, lastb=lastb: e.tensor_copy(out=kT[:, :, 0:128], in_=kT[:, :, lastb * 128:(lastb + 1) * 128]),
                 r=["kT"], w=["kT"])
            POOL(lambda e, lastb=lastb: e.tensor_copy(out=vE[:, 0, :, :], in_=vE[:, lastb, :, :]), r=["vE"], w=["vE"])
            POOL(lambda e, n=n: e.tensor_copy(out=sA[:, 0:64].rearrange("c (g t) -> c g t", g=4),
                                              in_=ubuf[:, :, n:n + 16]), r=["ubuf"], w=["sA"])
            POOL(lambda e: e.tensor_copy(out=ubuf[:, :, 0:16], in_=sA[:, 0:64].rearrange("c (g t) -> c g t", g=4)),
                 r=["sA"], w=["ubuf"])

    issue_conv(16)
    S.mark("phaseA")
    ENG = ["pe", "act", "dve", "pool", "sp"]
    S.barrier(ENG)
    while len(ctx) > ctxA_start:
        ctx.pop().__exit__(None, None, None)

    wg = sb("wg", [128, 8, D], BF16)
    wp = sb("wp", [128, 2, D], BF16)
    bg = sb("bg", [1, D], BF16)
    NRING = 3
    ring = [sb("ring%d" % i, [128, 8, 512], BF16) for i in range(NRING)]
    hT2 = sb("hT2", [128, 8, 512], BF16)
    hT3 = sb("hT3", [128, 8, 512], BF16)
    h1T = sb("h1T", [128, 32, 512], BF16)
    hr = [sb("hr%d" % i, [128, 512], BF16) for i in range(2)]
    ffbuf = sb("ffbuf", [128, 4, D])
    peb = sb("peb", [128, 4, 256], BF16)
    peT = sb("peT", [128, 4, 2, 128], BF16)
    xn2 = sb("xn2", [128, D], BF16)
    xnb = [xn, xn2]

    DMA_SP(gtab[:], g_post_mlp.partition_broadcast(128), "d_misc", w=["gtab"])
    wg_v = w_gate.rearrange("(k p) n -> p k n", p=128)
    wp_v = w_ple.rearrange("(k p) n -> p k n", p=128)

    bgroups = [[0, 1, 2, 3], [4, 5, 6, 7], [8, 9, 10, 11], [12, 13, 14], [15, 16]]
    blocks = []
    for gi in range(len(bgroups)):
        for b in range(8):
            blocks.append((gi, "up", b))
        for b in range(8):
            blocks.append((gi, "dn", b))
    issued = [0]

    def issue_ring(upto):
        while issued[0] < min(upto, len(blocks)):
            i = issued[0]
            gi_, kind, b = blocks[i]
            buf = ring[i % NRING]
            src = (scr_up if kind == "up" else scr_dn)[b, :, :]
            DMA_SP(buf[:].rearrange("p k n -> p (k n)"), src, "d_ring%d" % (i % NRING), r=ALL_SCR,
                   w=[("ring", i % NRING)])
            issued[0] += 1

    S.mark("pB0")
    issue_ring(NRING)
    S.mark("pB1")
    bi = 0
    xn_i = [0]

    def xnh(ti):
        return hT3[:, :, ti * 128:(ti + 1) * 128]

    def B1a(gi):
        tiles = bgroups[gi]
        for ti, t in enumerate(tiles):
            s0 = stat_slot(1)
            xb = xnh(ti)
            ACT(lambda e, t=t, s0=s0: e.activation(out=junk[:], in_=xall[:, t, :], func=AF.Square,
                                                   accum_out=stat[:, s0:s0 + 1]), r=[("x", t)], w=["junk", ("st", s0)])
            rstd_op(stat[:, s0:s0 + 1], 1, D, stat[:, s0:s0 + 1], [("st", s0)], [("st", s0)])
            DVE(lambda e, t=t, s0=s0, xb=xb: e.tensor_scalar(out=xb, in0=xall[:, t, :].rearrange("p (k c) -> p k c", k=8),
                                                             scalar1=stat[:, s0:s0 + 1], scalar2=None, op0=ALU.mult),
                r=[("x", t), ("st", s0)], w=[("hT3", ti)])

    def B1b(gi):
        tiles = bgroups[gi]
        for ti, t in enumerate(tiles):
            xb = xnh(ti)
            bank = 6 + ti % 2
            bt = ps[:, bank, :].bitcast(BF16)

            def f(e, xb=xb, bt=bt):
                last = None
                for k in range(8):
                    last = e.transpose(out=bt[:, k * 128:(k + 1) * 128], in_=xb[:, k, :], identity=ident_b[:])
                return last
            PE(f, r=[("hT3", ti), "ident_b"], w=[("ps", bank)])
            btv = bt.rearrange("p (k t) -> p k t", k=8)
            DVE(lambda e, btv=btv, ti=ti: e.tensor_tensor(out=hT2[:, :, ti * 128:(ti + 1) * 128], in0=btv,
                                                          in1=gcol[:, 8:16].unsqueeze(2).broadcast_to([128, 8, 128]),
                                                          op=ALU.mult), r=[("ps", bank), "gcol"], w=[("hT2", ti)])

    def PEload(gi):
        tiles = bgroups[gi]
        for ti, t in enumerate(tiles):
            pb_ = peb[:, ti, :]
            DMA_POOL(pb_, pein[t, :, :], "d_pe%d" % ti, w=[("peb", ti)])
        for ti, t in enumerate(tiles):
            pb_ = peb[:, ti, :]
            bank = 6 + ti % 2
            btp = ps[:, bank, :].bitcast(BF16)

            def fpet(e, btp=btp, pb_=pb_):
                last = None
                for k in range(2):
                    last = e.transpose(out=btp[:, k * 128:(k + 1) * 128], in_=pb_[:, k * 128:(k + 1) * 128],
                                       identity=ident_b[:])
                return last
            PE(fpet, r=[("peb", ti), "ident_b"], w=[("ps", bank)])
            ACT(lambda e, btp=btp, ti=ti: e.activation(out=peT[:, ti, :, :],
                                                       in_=btp[:, 0:256].rearrange("p (k t) -> p k t", k=2),
                                                       func=AF.Copy), r=[("ps", bank)], w=[("peT", ti)])

    B1a(0)
    B1b(0)
    for k in range(0, 8, 4):
        DMA_POOL(wg[:, k:k + 4, :], wg_v[:, k:k + 4, :], "d_wg", w=[("wg", k)])
    DMA_POOL(wp[:], wp_v[:, :, :], "d_wp", w=["wp"])
    DMA_POOL(bg[:], b_gate[:, :], "d_bg", w=["bg"])
    bi_box = [0]

    def B2_block(gi, b):
        tiles = bgroups[gi]
        nt_ = len(tiles)
        N = nt_ * 128
        bi = bi_box[0]
        issue_ring(bi + NRING)
        buf = ring[bi % NRING]
        bres = ("ring", bi % NRING)
        for mi in range(4):
            m = b * 4 + mi
            bank = 5 + m % 2

            def fup(e, buf=buf, mi=mi, bank=bank, N=N):
                last = None
                for k in range(8):
                    last = e.matmul(ps[:, bank, 0:N], lhsT=buf[:, k, mi * 128:(mi + 1) * 128], rhs=hT2[:, k, 0:N],
                                    start=(k == 0), stop=(k == 7))
                return last
            PE(fup, r=[bres] + [("hT2", ti) for ti in range(nt_)], w=[("ps", bank)])
            hrt = hr[m % 2]
            ACT(lambda e, bank=bank, hrt=hrt, N=N: e.activation(out=hrt[:, 0:N], in_=ps[:, bank, 0:N], func=AF.Relu),
                r=[("ps", bank)], w=[("hr", m % 2)])
            DVE(lambda e, m=m, hrt=hrt, N=N: e.tensor_tensor(out=h1T[:, m, 0:N], in0=hrt[:, 0:N], in1=hrt[:, 0:N],
                                                             op=ALU.mult), r=[("hr", m % 2)], w=[("h1T", m)])
        bi_box[0] += 1

    def B3(gi, sbase):
        tiles = bgroups[gi]
        nt_ = len(tiles)
        if gi + 1 < len(bgroups):
            B1a(gi + 1)
        for n_ in range(2):
            if n_ == 1:
                if gi + 1 < len(bgroups):
                    B1b(gi + 1)
                PEload(gi)
            for cb in range(4):
                bi = bi_box[0]
                issue_ring(bi + NRING)
                buf = ring[bi % NRING]
                bres = ("ring", bi % NRING)
                for ti in range(nt_):
                    def fdn(e, buf=buf, cb=cb, ti=ti):
                        last = None
                        for ci in range(8):
                            last = e.matmul(ps[:, ti, :], lhsT=h1T[:, cb * 8 + ci, ti * 128:(ti + 1) * 128],
                                            rhs=buf[:, ci, :], start=(cb == 0 and ci == 0), stop=(cb == 3 and ci == 7))
                        return last
                    PE(fdn, r=[bres] + [("h1T", cb * 8 + ci) for ci in range(8)], w=[("ps", ti)])
                bi_box[0] += 1
            for ti in range(nt_):
                sl = sbase + ti * 2 + n_
                ACT(lambda e, ti=ti, sl=sl: e.activation(out=junk[:, 0:512], in_=ps[:, ti, :], func=AF.Square,
                                                         accum_out=stat[:, sl:sl + 1]), r=[("ps", ti)], w=["junk", ("st", sl)])
                DVE(lambda e, ti=ti, n_=n_: e.tensor_copy(out=ffbuf[:, ti, n_ * 512:(n_ + 1) * 512], in_=ps[:, ti, :]),
                    r=[("ps", ti)], w=[("ff", ti)])

    def B4a(ti, t, sbase):
        sl = sbase + ti * 2
        POOL(lambda e, sl=sl: e.tensor_tensor(out=stat[:, sl:sl + 1], in0=stat[:, sl:sl + 1], in1=stat[:, sl + 1:sl + 2],
                                              op=ALU.add), r=[("st", sl), ("st", sl + 1)], w=[("st", sl)])
        rstd_op(stat[:, sl:sl + 1], 1, D, stat[:, sl:sl + 1], [("st", sl)], [("st", sl)])
        DVE(lambda e, ti=ti, sl=sl: e.scalar_tensor_tensor(out=ffbuf[:, ti, :], in0=ffbuf[:, ti, :],
                                                           scalar=stat[:, sl:sl + 1], in1=gtab[:], op0=ALU.mult,
                                                           op1=ALU.mult),
            r=[("ff", ti), ("st", sl), "gtab"], w=[("ff", ti)])
        DVE(lambda e, t=t, ti=ti: e.tensor_tensor(out=xall[:, t, :], in0=xall[:, t, :], in1=ffbuf[:, ti, :], op=ALU.add),
            r=[("ff", ti), ("x", t)], w=[("x", t)])
        xb = xnb[xn_i[0] % 2]
        xres = "xn%d" % (xn_i[0] % 2)
        xn_i[0] += 1
        ACT(lambda e, t=t, xb=xb: e.activation(out=xb[:], in_=xall[:, t, :], func=AF.Copy), r=[("x", t)], w=[xres])
        b4x[ti] = (xb, xres)

    b4x = {}

    def B4a_pe(ti):
        xb, xres = b4x[ti]
        transposes8(xb, xres, 7, hT3[:, :, ti * 128:(ti + 1) * 128], ("hT3", ti), gain_ap=None)

    def B4b(ti, t):
        gb = (ti % 2) * 2
        thb = ffbuf[:, ti, :]
        pbk = [4, 7]

        def fgate(e, ti=ti, gb=gb):
            last = None
            for n_ in range(2):
                for k in range(8):
                    e.matmul(ps[:, gb + n_, :], lhsT=hT3[:, k, ti * 128:(ti + 1) * 128],
                             rhs=wg[:, k, n_ * 512:(n_ + 1) * 512], start=(k == 0), stop=False)
                last = e.matmul(ps[:, gb + n_, :], lhsT=ones_row[0:1, :], rhs=bg[0:1, n_ * 512:(n_ + 1) * 512],
                                start=False, stop=True)
            return last
        PE(fgate, r=[("hT3", ti), ("wg", 0), ("wg", 4), "bg", "ones_row"], w=[("ps", gb), ("ps", gb + 1)])
        ACT(lambda e, gb=gb, thb=thb: e.activation(out=thb.rearrange("p (a b) -> p a b", a=2), in_=ps[:, gb:gb + 2, :],
                                                   func=AF.Tanh, scale=0.5),
            r=[("ps", gb), ("ps", gb + 1)], w=[("ff", ti)])
        for n_ in range(2):
            def fple(e, ti=ti, n_=n_):
                last = None
                for k in range(2):
                    last = e.matmul(ps[:, pbk[n_], :], lhsT=peT[:, ti, k, :], rhs=wp[:, k, n_ * 512:(n_ + 1) * 512],
                                    start=(k == 0), stop=(k == 1))
                return last
            PE(fple, r=[("peT", ti), "wp"], w=[("ps", pbk[n_])])
        for n_ in range(2):
            DVE(lambda e, n_=n_, thb=thb: e.scalar_tensor_tensor(out=thb[:, n_ * 512:(n_ + 1) * 512],
                                                                 in0=thb[:, n_ * 512:(n_ + 1) * 512], scalar=1.0,
                                                                 in1=ps[:, pbk[n_], :], op0=ALU.add, op1=ALU.mult),
                r=[("ff", ti), ("ps", pbk[n_])], w=[("ff", ti)])
        POOL(lambda e, t=t, thb=thb: e.tensor_tensor(out=xall[:, t, :], in0=xall[:, t, :], in1=thb, op=ALU.add),
             r=[("ff", ti), ("x", t)], w=[("x", t)])
        DMA_SP(y[t, :, :], xall[:, t, :], "d_out", r=[("x", t)], w=[("y", t)])

    for b in range(8):
        B2_block(0, b)
    DVE(lambda e: e.tensor_scalar(out=wp[:], in0=wp[:], scalar1=0.5, scalar2=None, op0=ALU.mult), r=["wp"], w=["wp"])
    for gi, tiles in enumerate(bgroups):
        nt_ = len(tiles)
        S.mark("b%d_B2" % gi)
        sbase = stat_slot(8)
        B3(gi, sbase)
        S.mark("b%d_B3" % gi)
        ups = list(range(8)) if gi + 1 < len(bgroups) else []
        order = [("el", ti) for ti in range(min(2, nt_))] + [("up", 0)]
        for ti in range(nt_):
            order.append(("pe", ti))
            if ti + 2 < nt_:
                order.append(("el", ti + 2))
            if ti % 2 == 1:
                order.append(("up", 0))
            if ti >= 2:
                order += [("b", ti - 2), ("up", 0)]
        for ti in range(max(0, nt_ - 2), nt_):
            order += [("b", ti), ("up", 0)]
        for (kind, ti) in order:
            if kind == "el":
                B4a(ti, tiles[ti], sbase)
            elif kind == "pe":
                B4a_pe(ti)
            elif kind == "b":
                B4b(ti, tiles[ti])
            elif ups:
                B2_block(gi + 1, ups.pop(0))
        while ups:
            B2_block(gi + 1, ups.pop(0))

    import os
    kstop = os.environ.get("KSTOP")
    if kstop:
        S.ops = S.ops[:S.marks[kstop]]
    if os.environ.get("KMAX"):
        S.ops = S.ops[:int(os.environ["KMAX"])]
        print("ops kept:", [(o["eng"], o["dma"]) for o in S.ops][-6:])
    semkeys = S.finalize()
    sem_cms = {}
    sems = {}
    for k in semkeys:
        cm = nc.semaphore("s_" + str(k))
        sems[k] = cm.__enter__()
        sem_cms[k] = cm
    final_waits = {}
    for op in S.ops:
        if op["dma"] is not None and op["sig"]:
            final_waits[op["semkey"]] = max(final_waits.get(op["semkey"], 0), op["sigval"])

    with nc.Block() as block:
        def emit(engname, eng):
            for op in S.ops:
                if op["eng"] != engname:
                    continue
                for (k, v) in op["waits"]:
                    eng.wait_ge(sems[k], v)
                if op["fn"] is None:
                    continue
                inst = op["fn"](eng)
                if op["sig"]:
                    inst.then_inc(sems[op["semkey"]], 16 if op["dma"] is not None else 1)
            if engname == "sp":
                for k, v in final_waits.items():
                    eng.wait_ge(sems[k], v)

        @block.tensor
        def _(e):
            emit("pe", e)

        @block.scalar
        def _(e):
            emit("act", e)

        @block.vector
        def _(e):
            emit("dve", e)

        @block.gpsimd
        def _(e):
            emit("pool", e)

        @block.sync
        def _(e):
            emit("sp", e)

    while ctx:
        ctx.pop().__exit__(None, None, None)
    psc.__exit__(None, None, None)
    for cm in sem_cms.values():
        cm.__exit__(None, None, None)
    return nc


def _win_perm():
    cols = []

    def head_of(p, r):
        return (r % 2) + 4 * (r // 2) + 2 * p
    for p in range(2):
        for half in range(2):
            for r in range(4):
                h = head_of(p, r)
                cols += list(range(h * 64 + half * 32, h * 64 + half * 32 + 32))
    for half in range(2):
        for r in range(4):
            g = r // 2
            cols += list(range(512 + g * 64 + half * 32, 512 + g * 64 + half * 32 + 32))
    cols += list(range(768, 1280))
    cols += list(range(640, 768))
    return np.array(cols, dtype=np.int64)


_NC_CACHE = {}


def kernel(x_prompt, x_sample, state_k, state_v, state_pool, p_prompt, p_sample, w_in, attn_sinks, pool_w,
           pool_scale, g_attn_out, g_pool_out, w_out, g_pre_mix, g_post_mix, g_pre_mlp, g_post_mlp, w_up, w_down,
           w_ple, w_ple_gate, b_ple_gate):
    f = lambda a: np.ascontiguousarray(np.asarray(a, dtype=np.float32))
    x_prompt, x_sample, state_k, state_v, state_pool = map(f, (x_prompt, x_sample, state_k, state_v, state_pool))
    p_prompt, p_sample = f(p_prompt), f(p_sample)
    if "nc" not in _NC_CACHE:
        _NC_CACHE["nc"] = build_nc()
    nc = _NC_CACHE["nc"]

    perm = _win_perm()
    w_in_p = f(f(w_in)[0][:, perm])

    def colv(g):
        return f(g)[0].reshape(8, 128).T
    gcols = f(np.concatenate([colv(g_pre_mix), colv(g_pre_mlp),
                              np.concatenate([f(g_attn_out)[0], f(g_pool_out)[0]]).reshape(8, 128).T], axis=1))
    kk = np.arange(128)[:, None]
    qq = np.arange(128)[None, :]
    m_prev = (kk > qq).astype(np.float32)
    m_cur = (kk <= qq).astype(np.float32)
    m_bd = ((kk // 8 == qq // 8) & (kk % 8 <= qq % 8)).astype(np.float32)
    smask = np.zeros((128, 18, 128), np.float32)
    for b in range(16):
        smask[:, b, :] = ((qq // 8 == b) & (kk > (qq % 8))).astype(np.float32)
    smask[:, 16, :] = m_bd
    inv = (np.float32(10000.0) ** (-np.arange(32, dtype=np.float32) / np.float32(32))).astype(np.float32)
    invf = f(np.tile(inv, 4).reshape(128, 1))
    poolw_l = f(np.transpose(f(pool_w)[0], (1, 0, 2)))

    in_maps = []
    for c in range(NCORES):
        xin = np.zeros((NT + 1, 128, D), np.float32)
        if c > 0:
            xin[0] = x_prompt[0, c * 2048 - 128:c * 2048]
        xin[1:17] = x_prompt[0, c * 2048:(c + 1) * 2048].reshape(16, 128, D)
        xin[17] = x_sample[c * 16:(c + 1) * 16].reshape(128, D)
        pein = np.zeros((NT, 128, 256), np.float32)
        pein[0:16] = p_prompt[0, 0, c * 2048:(c + 1) * 2048].reshape(16, 128, 256)
        pein[16] = p_sample[0, c * 16:(c + 1) * 16].reshape(128, 256)
        posv = np.zeros((NT + 1) * 128, np.float32)
        posv[0:128] = c * 2048 - 128 + np.arange(128)
        posv[128:17 * 128] = c * 2048 + np.arange(2048)
        posv[17 * 128:] = 16384 + (np.arange(128) % 8)
        if c == 0:
            posv[0:128] = 0.0
        masks = np.stack([m_prev, m_cur, m_prev if c > 0 else np.zeros_like(m_prev), m_bd], axis=1)
        invcnt = np.zeros((4, 16), np.float32)
        for g in range(4):
            w_ = 2 << g
            if c == 0:
                invcnt[g] = 1.0 / np.minimum(np.arange(16) + 1, w_)
            else:
                invcnt[g] = 1.0 / w_
        skc = state_k[0, c * 16:(c + 1) * 16]
        svc = state_v[0, c * 16:(c + 1) * 16]
        ab = np.zeros((16, 128, 2, 4, 32), np.float32)
        for half in range(2):
            for r in range(4):
                ab[:, :, half, r, :] = skc[:, :, r // 2, half * 32:(half + 1) * 32]
        sk_ab = np.transpose(ab.reshape(16, 128, 256), (1, 0, 2))
        svl = np.transpose(svc.reshape(16, 128, 128), (1, 0, 2))
        in_maps.append({
            "xin": xin, "pein": pein, "pos": posv.reshape(1, -1), "invf": invf, "w_in": w_in_p,
            "w_out": f(w_out)[0], "w_up": f(w_up)[0], "w_down": f(w_down)[0], "w_ple": f(w_ple)[0],
            "w_gate": f(w_ple_gate)[0], "b_gate": f(b_ple_gate), "pool_w": poolw_l, "pool_scale": f(pool_scale),
            "sinks": f(attn_sinks), "gcols": gcols, "g_post_mix": f(g_post_mix), "g_post_mlp": f(g_post_mlp),
            "masks": f(masks), "smask": smask, "invcnt": invcnt.reshape(1, 64), "sk_ab": f(sk_ab), "sv": f(svl),
            "sk_raw": f(skc.reshape(16, 128, 128)), "sv_raw": f(svc.reshape(16, 128, 128)),
            "spool": f(state_pool[0, c * 16:(c + 1) * 16]),
        })
    res = run_bass_kernel_spmd(nc, in_maps, core_ids=list(range(NCORES)))
    R = res.results
    y_prompt = np.concatenate([R[c]["y"][0:16].reshape(2048, D) for c in range(NCORES)], axis=0)[None]
    y_sample = np.concatenate([R[c]["y"][16].reshape(16, 8, D) for c in range(NCORES)], axis=0)
    nkp = R[7]["kout_p"].reshape(1, 1, 128, 2, 64)
    nvp = R[7]["vout_p"].reshape(1, 1, 128, 2, 64)
    nup = R[7]["uout_p"].reshape(1, 1, 15, 512)
    nks = np.concatenate([R[c]["kout_s"] for c in range(NCORES)], axis=0).reshape(1, 128, 128, 2, 64)
    nvs = np.concatenate([R[c]["vout_s"] for c in range(NCORES)], axis=0).reshape(1, 128, 128, 2, 64)
    nus = np.concatenate([R[c]["uout_s"] for c in range(NCORES)], axis=0).reshape(1, 128, 15, 512)
    out = (y_prompt, y_sample, nkp, nvp, nup, nks, nvs, nus)
    return tuple(np.ascontiguousarray(o, dtype=np.float32) for o in out)
```

```python
import math
import numpy as np
import concourse.bass as bass
import concourse.mybir as mybir
from concourse.bass_utils import run_bass_kernel_spmd

F32 = mybir.dt.float32
BF16 = mybir.dt.bfloat16
I32 = mybir.dt.int32
AF = mybir.ActivationFunctionType
ALU = mybir.AluOpType

NCORES = 8
D = 1024
NTP = 16
NT = NTP + 1
EPS = 1e-6
WIN_COLS = 1408
PI = math.pi


class Sched:
    def __init__(self):
        self.ops = []
        self.last_w = {}
        self.readers = {}

    def add(self, eng, fn, r=(), w=(), dma=None):
        r, w = list(r), list(w)
        for x in list(r):
            if isinstance(x, tuple) and x[0] == "ps":
                r.remove(x)
                if x not in w:
                    w.append(x)
        idx = len(self.ops)
        deps = set()
        for x in list(r) + list(w):
            if x in self.last_w:
                deps.add(self.last_w[x])
        for x in w:
            for rd in self.readers.get(x, ()):
                deps.add(rd)
        deps.discard(idx)
        self.ops.append(dict(eng=eng, fn=fn, deps=deps, dma=dma, sig=(dma is not None)))
        for x in w:
            self.last_w[x] = idx
            self.readers[x] = []
        for x in r:
            self.readers.setdefault(x, []).append(idx)
        return idx

    def mark(self, name):
        self.marks = getattr(self, "marks", {})
        self.marks[name] = len(self.ops)

    def barrier(self, engines):
        alldeps = set(range(len(self.ops)))
        for e in engines:
            self.ops.append(dict(eng=e, fn=None, deps=set(alldeps), dma=None, sig=False))
        self.last_w = {}
        self.readers = {}

    def finalize(self):
        ops = self.ops
        for i, op in enumerate(ops):
            for d in op["deps"]:
                if ops[d]["eng"] == "pe" and op["eng"] == "pe" and ops[d]["dma"] is None:
                    continue
                ops[d]["sig"] = True
        cnt = {}
        for op in ops:
            if op["fn"] is None:
                op["sig"] = False
            if op["sig"]:
                key = op["dma"] if op["dma"] is not None else op["eng"]
                inc = 16 if op["dma"] is not None else 1
                cnt[key] = cnt.get(key, 0) + inc
                op["semkey"] = key
                op["sigval"] = cnt[key]
        waited = {}
        for op in ops:
            e = op["eng"]
            need = {}
            for d in op["deps"]:
                dop = ops[d]
                if not dop["sig"]:
                    continue
                if dop["eng"] == "pe" and e == "pe" and dop["dma"] is None:
                    continue
                k = dop["semkey"]
                need[k] = max(need.get(k, 0), dop["sigval"])
            wl = []
            for k, v in need.items():
                if waited.get((e, k), 0) < v:
                    waited[(e, k)] = v
                    wl.append((k, v))
            op["waits"] = wl
        return sorted(cnt.keys())


def build_nc():
    nc = bass.Bass("TRN2", target_bir_lowering=False)
    S = Sched()

    def din(name, shape):
        return nc.dram_tensor(name, list(shape), F32, kind="ExternalInput").ap()

    def dout(name, shape):
        return nc.dram_tensor(name, list(shape), F32, kind="ExternalOutput").ap()

    xin = din("xin", [NT + 1, 128, D])
    pein = din("pein", [NT, 128, 256])
    pos = din("pos", [1, (NT + 1) * 128])
    invf = din("invf", [128, 1])
    w_in = din("w_in", [D, WIN_COLS])
    w_out = din("w_out", [D, D])
    w_up = din("w_up", [D, 4096])
    w_down = din("w_down", [4096, D])
    w_ple = din("w_ple", [256, D])
    w_gate = din("w_gate", [D, D])
    b_gate = din("b_gate", [1, D])
    pool_w = din("pool_w", [128, 4, 128])
    pool_scale = din("pool_scale", [1, 512])
    sinks = din("sinks", [1, 8])
    gcols = din("gcols", [128, 24])
    g_post_mix = din("g_post_mix", [1, D])
    g_post_mlp = din("g_post_mlp", [1, D])
    masks = din("masks", [128, 4, 128])
    smask = din("smask", [128, 18, 128])
    invcnt = din("invcnt", [1, 64])
    sk_ab = din("sk_ab", [128, 16, 256])
    sv = din("sv", [128, 16, 128])
    sk_raw = din("sk_raw", [16, 128, 128])
    sv_raw = din("sv_raw", [16, 128, 128])
    spool = din("spool", [16, 15, 512])

    y = dout("y", [NT, 128, D])
    kout_p = dout("kout_p", [128, 128])
    vout_p = dout("vout_p", [128, 128])
    uout_p = dout("uout_p", [15, 512])
    kout_s = dout("kout_s", [16, 128, 128])
    vout_s = dout("vout_s", [16, 128, 128])
    uout_s = dout("uout_s", [16, 15, 512])

    scr_up = nc.dram_tensor("scr_up", [8, 128, 4096], BF16, kind="Internal").ap()
    scr_dn = nc.dram_tensor("scr_dn", [8, 128, 4096], BF16, kind="Internal").ap()
    ctx = []

    def sb(name, shape, dt=F32):
        cm = nc.sbuf_tensor(name, list(shape), dt)
        t = cm.__enter__()
        ctx.append(cm)
        return t

    psc = nc.psum_tensor("ps", [128, 8, 512], F32)
    ps = psc.__enter__()
    xall = sb("xall", [128, NT, D])
    ident_b = sb("ident_b", [128, 128], BF16)
    ident_f = sb("ident_f", [128, 128])
    gcol = sb("gcol", [128, 24])
    stat = sb("stat", [128, 64])
    neghalf = sb("neghalf", [128, 8])
    junk = sb("junk", [128, D], BF16)
    xn = sb("xn", [128, D], BF16)
    gtab = sb("gtab", [128, D])
    tmpA = sb("tmpA", [128, D])
    ones_row = sb("ones_row", [1, 128], BF16)

    def PE(fn, r=(), w=()):
        return S.add("pe", fn, r, w)

    def ACT(fn, r=(), w=()):
        return S.add("act", fn, r, w)

    def DVE(fn, r=(), w=()):
        return S.add("dve", fn, r, w)

    def POOL(fn, r=(), w=()):
        return S.add("pool", fn, r, w)

    misc_i = [0]

    def DMA_SP(out, in_, sem, r=(), w=()):
        if sem == "d_misc":
            sem = "d_m%d" % misc_i[0]
            misc_i[0] += 1
        return S.add("sp", lambda e: e.dma_start(out=out, in_=in_), r, w, dma=sem)

    def DMA_POOL(out, in_, sem, r=(), w=()):
        return S.add("pool", lambda e: e.dma_start(out=out, in_=in_), r, w, dma=sem)

    stat_i = [0]

    def stat_slot(n=1):
        i = stat_i[0]
        if i + n > 64:
            i = 0
        stat_i[0] = i + n
        return i

    def rstd_op(ss_ap, n, dim, out_ap, rres, wres):
        POOL(lambda e: e.tensor_scalar(out=out_ap, in0=ss_ap, scalar1=1.0 / dim, scalar2=EPS,
                                       op0=ALU.mult, op1=ALU.add), r=rres, w=wres)
        POOL(lambda e: e.tensor_tensor(out=out_ap, in0=out_ap, in1=neghalf[:, 0:n], op=ALU.pow),
             r=list(wres) + ["neghalf"], w=wres)

    def transposes8(src_bf, src_res, bank, dst_ap, dst_res, gain_ap=None):
        bt = ps[:, bank, :].bitcast(BF16)

        def f(e):
            last = None
            for k in range(8):
                last = e.transpose(out=bt[:, k * 128:(k + 1) * 128], in_=src_bf[:, k * 128:(k + 1) * 128],
                                   identity=ident_b[:])
            return last
        PE(f, r=[src_res, "ident_b"], w=[("ps", bank)])
        btv = bt.rearrange("p (k t) -> p k t", k=8)
        if gain_ap is not None:
            DVE(lambda e: e.tensor_tensor(out=dst_ap, in0=btv, in1=gain_ap.broadcast_to([128, 8, 128]), op=ALU.mult),
                r=[("ps", bank), "gcol"], w=[dst_res])
        else:
            ACT(lambda e: e.activation(out=dst_ap, in_=btv, func=AF.Copy), r=[("ps", bank)], w=[dst_res])

    DMA_SP(gcol[:], gcols[:, :], "d_misc", w=["gcol"])
    POOL(lambda e: e.memset(neghalf[:], -0.5), w=["neghalf"])
    POOL(lambda e: e.memset(ones_row[:], 1.0), w=["ones_row"])
    POOL(lambda e: e.memset(ident_f[:], 0.0), w=["ident_f"])
    POOL(lambda e: e.affine_select(out=ident_f[:], in_=ident_f[:], pattern=[[-1, 128]], compare_op=ALU.not_equal,
                                   fill=1.0, base=0, channel_multiplier=1), r=["ident_f"], w=["ident_f"])
    POOL(lambda e: e.tensor_copy(out=ident_b[:], in_=ident_f[:]), r=["ident_f"], w=["ident_b"])


    S.mark("setup0")
    ctxA_start = len(ctx)
    wi = sb("wi", [128, 8, WIN_COLS], BF16)
    wo = sb("wo", [128, 8, D], BF16)
    pw = sb("pw", [128, 4, 128], BF16)
    hT = sb("hT", [128, 8, 640], BF16)
    qT = sb("qT", [128, 4, 512], BF16)
    kT = sb("kT", [128, 2, 640], BF16)
    kf = sb("kf", [128, 2, 128])
    vE = sb("vE", [128, 5, 2, 65], BF16)
    vf = sb("vf", [128, 128])
    ubuf = sb("ubuf", [128, 4, 528])
    sA = sb("sA", [128, 528])
    sB = sb("sB", [128, 528])
    rT = sb("rT", [128, 4, 512], BF16)
    cosT = sb("cosT", [128, 640])
    sinT = sb("sinT", [128, 640])
    rt = [sb("rt%d" % i, [128, 512]) for i in range(4)]
    PT = sb("PT", [128, 2, 4, 512], BF16)
    attn_f = sb("attn_f", [128, 512])
    cat = sb("cat", [128, D], BF16)
    catT = sb("catT", [128, 8, 128], BF16)
    maskb = sb("maskb", [128, 4, 128], BF16)
    mbias = sb("mbias", [128, 2, 512], BF16)
    mbs = sb("mbs", [128, 2, 512], BF16)
    smaskb = sb("smaskb", [128, 18, 128], BF16)
    esink = sb("esink", [128, 8])
    invcnt_b = sb("invcnt_b", [128, 4, 16])
    vsE = sb("vsE", [128, 16, 2, 65], BF16)
    us = sb("us", [128, 4, 16, 23])
    invf_t = sb("invf_t", [128, 1])
    ang = sb("ang", [128, 640])
    kq = sb("kq", [128, 640])
    kqi = kq[:].bitcast(I32)
    ksT = hT[:].rearrange("p k c -> p (k c)")[:, 0:4096].rearrange("p (a b k) -> p a b k", a=2, b=16)
    usA = sA[:, 0:368].rearrange("p (b t) -> p b t", b=16)
    usB = sB[:, 0:368].rearrange("p (b t) -> p b t", b=16)
    uo = tmpA[:, 0:512]
    pwf = attn_f[:].rearrange("p (g d) -> p g d", g=4)
    pscale_b = cat[:].bitcast(F32)

    wi_v = w_in.rearrange("(k p) n -> p k n", p=128)
    DMA_SP(invf_t[:], invf[:, :], "d_misc", w=["invf"])
    DMA_SP(gtab[:], g_post_mix.partition_broadcast(128), "d_misc", w=["gtab"])
    DMA_SP(pwf, pool_w[:, :, :], "d_misc", w=["attn_f"])
    DMA_SP(pscale_b, pool_scale.partition_broadcast(128), "d_misc", w=[("cat", 0), ("cat", 1)])
    DMA_SP(esink[:], sinks.partition_broadcast(128), "d_misc", w=["esink"])
    DMA_SP(invcnt_b[:].rearrange("p g t -> p (g t)"), invcnt.partition_broadcast(128), "d_misc", w=["invcnt_b"])
    DMA_SP(ang[:, 0:640], pos[:, 0:640].partition_broadcast(128), "d_ang", w=["ang"])
    DMA_SP(tmpA[:], xin[0, :, :], "d_x0", w=["tmpA"])
    DMA_SP(xall[:, 0, :], xin[1, :, :], "d_x1", w=[("x", 0)])
    DMA_POOL(maskb[:], masks[:, :, :], "d_maskb", w=["maskb"])
    for t in range(1, 4):
        DMA_SP(xall[:, t, :], xin[t + 1, :, :], "d_x%d" % (t + 1), w=[("x", t)])
    for si, lo in enumerate((0, 1)):
        DVE(lambda e, si=si, lo=lo: e.tensor_scalar(
            out=mbias[:, si, :].rearrange("k (p x) -> k p x", p=2),
            in0=maskb[:, lo:lo + 2, :].rearrange("k j q -> k (j q)").unsqueeze(1).broadcast_to([128, 2, 256]),
            scalar1=-1.0, scalar2=30000.0, op0=ALU.add, op1=ALU.mult), r=["maskb"], w=["mbias"])
    POOL(lambda e: e.memset(vsE[:], 1.0), w=["vsE"])

    def late_loads():
        DMA_POOL(smaskb[:], smask[:, :, :], "d_smaskb", w=["smaskb"])
        DMA_POOL(vsE[:, :, :, 0:64], sv.rearrange("p b (g d) -> p b g d", g=2), "d_vsE", w=["vsE"])
        DMA_SP(kout_s[:, 0:120, :], sk_raw[:, 8:128, :], "d_out", w=["kout_s_state"])
        DMA_SP(vout_s[:, 0:120, :], sv_raw[:, 8:128, :], "d_out", w=["vout_s_state"])
        DMA_SP(uout_s[:, 0:7, :], spool[:, 8:15, :], "d_out", w=["uout_s_state"])
    wu_v = w_up.rearrange("(k p) n -> p k n", p=128)
    wd_v = w_down.rearrange("(c p) n -> p c n", p=128)
    conv_list = []
    for b in range(8):
        conv_list.append((scr_up[b, :, :].rearrange("p (k n) -> p k n", k=8), wu_v[:, :, b * 512:(b + 1) * 512],
                          ("scr", "up", b)))
    for b in range(8):
        n_, cb = b // 4, b % 4
        conv_list.append((scr_dn[b, :, :].rearrange("p (k n) -> p k n", k=8),
                          wd_v[:, cb * 8:(cb + 1) * 8, n_ * 512:(n_ + 1) * 512], ("scr", "dn", b)))

    ALL_SCR = [c_[2] for c_ in conv_list]

    def issue_conv(n=1):
        for _ in range(n):
            if conv_list:
                o_, i_, res = conv_list.pop(0)
                DMA_POOL(o_, i_, "d_cv", w=[res])

    DVE(lambda e: e.tensor_tensor(out=pw[:], in0=pwf, in1=pscale_b.rearrange("p (g d) -> p g d", g=4),
                                  op=ALU.mult), r=["attn_f", ("cat", 0), ("cat", 1)], w=["pw"])
    ACT(lambda e: e.activation(out=esink[:], in_=esink[:], func=AF.Exp), r=["esink"], w=["esink"])
    POOL(lambda e: e.memset(vE[:], 1.0), w=["vE"])
    POOL(lambda e: e.memset(ubuf[:, :, 0:16], 0.0), w=["ubuf"])
    POOL(lambda e: e.memset(sA[:], 0.0), w=["sA"])
    POOL(lambda e: e.memset(sB[:], 0.0), w=["sB"])
    for k in range(8):
        DMA_POOL(wi[:, k, :], wi_v[:, k, :], "d_wi0" if k == 0 else "d_wi", r=["tmpA", ("x", 0)] if k == 0 else [],
                 w=[("wi", k)])
    wo_v = w_out.rearrange("(k p) n -> p k n", p=128)
    for k in range(0, 8, 4):
        DMA_POOL(wo[:, k:k + 4, :], wo_v[:, k:k + 4, :], "d_wo", w=[("wo", k)])

    C1 = 6.28125
    C2 = 2 * PI - C1

    def rope_tables(pc0, cw, tcol):
        DMA_SP(ang[:, 0:cw], pos[:, pc0:pc0 + cw].partition_broadcast(128), "d_ang", w=["ang"])
        DVE(lambda e: e.tensor_scalar(out=ang[:, 0:cw], in0=ang[:, 0:cw], scalar1=invf_t[:, 0:1], scalar2=None,
                                      op0=ALU.mult), r=["ang", "invf"], w=["ang"])
        DVE(lambda e: e.tensor_scalar(out=kq[:, 0:cw], in0=ang[:, 0:cw], scalar1=1.0 / (2 * PI), scalar2=None,
                                      op0=ALU.mult), r=["ang"], w=["kq"])
        DVE(lambda e: e.tensor_copy(out=kqi[:, 0:cw], in_=kq[:, 0:cw]), r=["kq"], w=["kq"])
        DVE(lambda e: e.tensor_copy(out=kq[:, 0:cw], in_=kqi[:, 0:cw]), r=["kq"], w=["kq"])
        DVE(lambda e: e.scalar_tensor_tensor(out=ang[:, 0:cw], in0=kq[:, 0:cw], scalar=-C1, in1=ang[:, 0:cw],
                                             op0=ALU.mult, op1=ALU.add), r=["kq", "ang"], w=["ang"])
        DVE(lambda e: e.scalar_tensor_tensor(out=ang[:, 0:cw], in0=kq[:, 0:cw], scalar=-C2, in1=ang[:, 0:cw],
                                             op0=ALU.mult, op1=ALU.add), r=["kq", "ang"], w=["ang"])
        DVE(lambda e: e.tensor_scalar(out=kq[:, 0:cw], in0=ang[:, 0:cw], scalar1=-PI, scalar2=PI, op0=ALU.max,
                                      op1=ALU.min), r=["ang"], w=["kq"])
        ACT(lambda e: e.activation(out=sinT[:, tcol:tcol + cw], in_=kq[:, 0:cw], func=AF.Sin), r=["kq"], w=["sinT"])
        DVE(lambda e: e.tensor_scalar(out=kq[:, 0:cw], in0=ang[:, 0:cw], scalar1=PI / 2, scalar2=None, op0=ALU.add),
            r=["ang"], w=["kq"])
        DVE(lambda e: e.tensor_scalar(out=ang[:, 0:cw], in0=kq[:, 0:cw], scalar1=PI, scalar2=2 * PI, op0=ALU.is_gt,
                                      op1=ALU.mult), r=["kq"], w=["ang"])
        DVE(lambda e: e.tensor_tensor(out=kq[:, 0:cw], in0=kq[:, 0:cw], in1=ang[:, 0:cw], op=ALU.subtract),
            r=["kq", "ang"], w=["kq"])
        DVE(lambda e: e.tensor_scalar(out=kq[:, 0:cw], in0=kq[:, 0:cw], scalar1=-PI, scalar2=PI, op0=ALU.max,
                                      op1=ALU.min), r=["kq"], w=["kq"])
        ACT(lambda e: e.activation(out=cosT[:, tcol:tcol + cw], in_=kq[:, 0:cw], func=AF.Sin), r=["kq"], w=["cosT"])

    def rope_tables_split(pc0, cw, tcol, sarg=None, sres="sA", dma=True):
        if sarg is None:
            sarg = sA[:, 0:cw]

        def part0():
            DMA_SP(ang[:, 0:cw], pos[:, pc0:pc0 + cw].partition_broadcast(128), "d_ang", w=["ang"])

        def part1():
            DVE(lambda e: e.tensor_scalar(out=ang[:, 0:cw], in0=ang[:, 0:cw], scalar1=invf_t[:, 0:1], scalar2=None,
                                          op0=ALU.mult), r=["ang", "invf"], w=["ang"])
            DVE(lambda e: e.tensor_scalar(out=kq[:, 0:cw], in0=ang[:, 0:cw], scalar1=1.0 / (2 * PI), scalar2=None,
                                          op0=ALU.mult), r=["ang"], w=["kq"])
            DVE(lambda e: e.tensor_copy(out=kqi[:, 0:cw], in_=kq[:, 0:cw]), r=["kq"], w=["kq"])
            DVE(lambda e: e.tensor_copy(out=kq[:, 0:cw], in_=kqi[:, 0:cw]), r=["kq"], w=["kq"])
            DVE(lambda e: e.scalar_tensor_tensor(out=ang[:, 0:cw], in0=kq[:, 0:cw], scalar=-C1, in1=ang[:, 0:cw],
                                                 op0=ALU.mult, op1=ALU.add), r=["kq", "ang"], w=["ang"])
            DVE(lambda e: e.scalar_tensor_tensor(out=ang[:, 0:cw], in0=kq[:, 0:cw], scalar=-C2, in1=ang[:, 0:cw],
                                                 op0=ALU.mult, op1=ALU.add), r=["kq", "ang"], w=["ang"])
            DVE(lambda e: e.tensor_scalar(out=sarg, in0=ang[:, 0:cw], scalar1=-PI, scalar2=PI, op0=ALU.max,
                                          op1=ALU.min), r=["ang"], w=[sres])

        def part2():
            DVE(lambda e: e.tensor_scalar(out=kq[:, 0:cw], in0=ang[:, 0:cw], scalar1=PI / 2, scalar2=None, op0=ALU.add),
                r=["ang"], w=["kq"])
            DVE(lambda e: e.tensor_scalar(out=ang[:, 0:cw], in0=kq[:, 0:cw], scalar1=PI, scalar2=2 * PI, op0=ALU.is_gt,
                                          op1=ALU.mult), r=["kq"], w=["ang"])
            DVE(lambda e: e.tensor_tensor(out=kq[:, 0:cw], in0=kq[:, 0:cw], in1=ang[:, 0:cw], op=ALU.subtract),
                r=["kq", "ang"], w=["kq"])
            DVE(lambda e: e.tensor_scalar(out=kq[:, 0:cw], in0=kq[:, 0:cw], scalar1=-PI, scalar2=PI, op0=ALU.max,
                                          op1=ALU.min), r=["kq"], w=["kq"])

        def part3():
            ACT(lambda e: e.activation(out=sinT[:, tcol:tcol + cw], in_=sarg, func=AF.Sin), r=[sres], w=["sinT"])
            ACT(lambda e: e.activation(out=cosT[:, tcol:tcol + cw], in_=kq[:, 0:cw], func=AF.Sin), r=["kq"], w=["cosT"])
        if dma:
            part0()
        return [part1, part2, part3]

    rp0 = rope_tables_split(0, 640, 0, sarg=ubuf[:, 0:2, :].rearrange("p g c -> p (g c)")[:, 0:640], sres="ubuf",
                            dma=False)
    rp0[0]()
    rp0[1]()
    for t in range(4, NT):
        DMA_SP(xall[:, t, :], xin[t + 1, :, :], "d_x%d" % (t + 1), r=[("wi", 7)], w=[("x", t)])

    def load_spool():
        for half in range(2):
            spf = rt[3][0:120, :]
            DMA_SP(spf, spool[half * 8:(half + 1) * 8, :, :].rearrange("b i c -> (b i) c"), "d_misc", w=["rt3"])
            for g in range(4):
                bank = 6 + (half * 4 + g) % 2
                PE(lambda e, g=g, bank=bank, spf=spf: e.transpose(out=ps[:, bank, 0:120], in_=spf[:, g * 128:(g + 1) * 128],
                                                                  identity=ident_f[0:120, 0:120]),
                   r=["rt3", "ident_f"], w=[("ps", bank)])
                ACT(lambda e, half=half, g=g, bank=bank: e.activation(
                    out=us[:, g, half * 8:(half + 1) * 8, 0:15],
                    in_=ps[:, bank, 0:120].rearrange("p (b i) -> p b i", b=8), func=AF.Copy),
                    r=[("ps", bank)], w=["us"])

    def state_k_load(q4):
        rtb = rt[q4 % 3][:].bitcast(BF16).rearrange("p (b c) -> p b c", b=4)
        DMA_POOL(rtb, sk_ab[:, q4 * 4:(q4 + 1) * 4, :], "d_sk%d" % q4, w=["rt%d" % (q4 % 3)])

    def build_state_kT():
        for q4 in range(4):
            rtb = rt[q4 % 3][:].bitcast(BF16).rearrange("p (b c) -> p b c", b=4)
            rres = "rt%d" % (q4 % 3)
            if q4 == 3:
                state_k_load(q4)
            for bb in range(4):
                b = q4 * 4 + bb
                for ab in range(2):
                    bank = 6 + ab
                    bt = ps[:, bank, :].bitcast(BF16)
                    PE(lambda e, bb=bb, ab=ab, bt=bt, rtb=rtb: e.transpose(
                        out=bt[:, 0:128], in_=rtb[:, bb, ab * 128:(ab + 1) * 128], identity=ident_b[:]),
                       r=[rres, "ident_b"], w=[("ps", bank)])
                    ACT(lambda e, b=b, ab=ab, bt=bt: e.activation(out=ksT[:, ab, b, :], in_=bt[:, 0:128], func=AF.Copy),
                        r=[("ps", bank)], w=["hT"])

    S.mark("setup")
    groups = [[0, 1, 2, 3, 4], [5, 6, 7, 8], [9, 10, 11, 12], [13, 14, 15, 16], [17]]
    PAIRS = [(0, 1, "q", 0), (2, 3, "q", 2), (4, 5, "k", 0)]

    def head_of(p, r):
        return (r % 2) + 4 * (r // 2) + 2 * p

    def attention(tile, qcol, kblocks, set_i, ob0, pe_mask=None, pe_mask_fn=None, hook=None):
        nb = len(kblocks)
        nr = (nb + 1) // 2
        first_o = [True, True]
        for rd in range(nr):
            blks = kblocks[rd * 2: rd * 2 + 2]
            nj = len(blks)
            if pe_mask_fn is not None:
                pe_mask = pe_mask_fn(rd)
            for r in range(4):
                sbk = r % 2

                def fqk(e, r=r, blks=blks, sbk=sbk, pe_mask=pe_mask):
                    last = None
                    first = True
                    if pe_mask is not None:
                        e.matmul(ps[:, sbk, :], lhsT=ident_b[:], rhs=pe_mask, start=True, stop=False,
                                 skip_group_check=True)
                        first = False
                    for p in range(2):
                        for j, blk in enumerate(blks):
                            o = ps[:, sbk, (p * 2 + j) * 128:(p * 2 + j + 1) * 128]
                            e.matmul(o, lhsT=blk[0](r), rhs=qT[32 * r:32 * r + 32, 2 * p, qcol:qcol + 128],
                                     start=first, stop=False, tile_position=(32 * r, 0), skip_group_check=True)
                            first = False
                            last = e.matmul(o, lhsT=blk[1](r), rhs=qT[32 * r:32 * r + 32, 2 * p + 1, qcol:qcol + 128],
                                            start=False, stop=True, tile_position=(32 * r, 0), skip_group_check=True)
                    return last
                PE(fqk, r=["qT", "mbias", ("mbs", rd % 2), "ident_b"] + [b_[4] for b_ in blks], w=[("ps", sbk)])
                pt = PT[:, set_i, r, :]
                ACT(lambda e, sbk=sbk, pt=pt: e.activation(out=pt, in_=ps[:, sbk, :], func=AF.Exp, scale=0.125),
                    r=[("ps", sbk)], w=[("PT", set_i, r)])
                if pe_mask is None:
                    if nj == 2:
                        m2 = blks[0][3]
                        POOL(lambda e, pt=pt, m2=m2: e.tensor_tensor(
                            out=pt.rearrange("k (p j q) -> k p (j q)", p=2, j=2),
                            in0=pt.rearrange("k (p j q) -> k p (j q)", p=2, j=2),
                            in1=m2.rearrange("k j q -> k (j q)").unsqueeze(1).broadcast_to([128, 2, 256]), op=ALU.mult),
                            r=[("PT", set_i, r), "maskb", "smaskb"], w=[("PT", set_i, r)])
                    else:
                        m1 = blks[0][3]
                        ptv = pt.rearrange("k (p j q) -> k p j q", p=2, j=2)[:, :, 0, :]
                        POOL(lambda e, ptv=ptv, m1=m1: e.tensor_tensor(
                            out=ptv, in0=ptv, in1=m1.unsqueeze(1).broadcast_to([128, 2, 128]), op=ALU.mult),
                            r=[("PT", set_i, r), "maskb", "smaskb"], w=[("PT", set_i, r)])
                if hook is not None:
                    hook(rd, r)
            for r in range(4):
                def fpv(e, r=r, blks=blks, rd=rd):
                    last = None
                    for p in range(2):
                        h = head_of(p, r)
                        ob = ob0 + h // 4
                        for j, blk in enumerate(blks):
                            st = first_o[h // 4] and True
                            first_o[h // 4] = False
                            last = e.matmul(ps[:, ob, (h % 4) * 65:(h % 4) * 65 + 65],
                                            lhsT=PT[:, set_i, r, (p * 2 + j) * 128:(p * 2 + j + 1) * 128],
                                            rhs=blk[2](r // 2), start=st, stop=(rd == nr - 1 and j == nj - 1),
                                            skip_group_check=True)
                    return last
                PE(fpv, r=[("PT", set_i, r)] + [b_[5] for b_ in blks], w=[("ps", ob0), ("ps", ob0 + 1)])

    def attn_normalize(ob0):
        s0 = stat_slot(8)
        den = stat[:, s0:s0 + 8]
        for ob in range(2):
            ov = ps[:, ob0 + ob, 0:260].rearrange("q (h d) -> q h d", h=4)
            DVE(lambda e, ob=ob, ov=ov: e.tensor_tensor(out=den[:, ob * 4:(ob + 1) * 4].unsqueeze(2), in0=ov[:, :, 64:65],
                                                        in1=esink[:, ob * 4:(ob + 1) * 4].unsqueeze(2), op=ALU.add),
                r=[("ps", ob0 + ob), "esink"], w=[("den", s0)])
        DVE(lambda e: e.reciprocal(out=den, in_=den), r=[("den", s0)], w=[("den", s0)])
        for ob in range(2):
            ov = ps[:, ob0 + ob, 0:260].rearrange("q (h d) -> q h d", h=4)
            DVE(lambda e, ob=ob, ov=ov: e.tensor_tensor(
                out=attn_f[:, ob * 256:(ob + 1) * 256].rearrange("q (h d) -> q h d", h=4), in0=ov[:, :, 0:64],
                in1=den[:, ob * 4:(ob + 1) * 4].unsqueeze(2).broadcast_to([128, 4, 64]), op=ALU.mult),
                r=[("ps", ob0 + ob), ("den", s0)], w=["attn_f"])

    def rope_pair(bankA, bankB, ccol, n, dstA, dstB, dres, f32dst=None):
        Cc = cosT[:, ccol:ccol + n]
        Sc = sinT[:, ccol:ccol + n]
        A = ps[:, bankA, 0:n]
        B = ps[:, bankB, 0:n]
        DVE(lambda e: e.tensor_tensor(out=rt[0][:, 0:n], in0=A, in1=Cc, op=ALU.mult), r=[("ps", bankA), "cosT"], w=["rt0"])
        DVE(lambda e: e.tensor_tensor(out=rt[1][:, 0:n], in0=B, in1=Sc, op=ALU.mult), r=[("ps", bankB), "sinT"], w=["rt1"])
        DVE(lambda e: e.tensor_tensor(out=rt[2][:, 0:n], in0=B, in1=Cc, op=ALU.mult), r=[("ps", bankB), "cosT"], w=["rt2"])
        DVE(lambda e: e.tensor_tensor(out=rt[3][:, 0:n], in0=A, in1=Sc, op=ALU.mult), r=[("ps", bankA), "sinT"], w=["rt3"])
        DVE(lambda e: e.tensor_tensor(out=dstA, in0=rt[0][:, 0:n], in1=rt[1][:, 0:n], op=ALU.subtract),
            r=["rt0", "rt1"], w=[dres])
        DVE(lambda e: e.tensor_tensor(out=dstB, in0=rt[2][:, 0:n], in1=rt[3][:, 0:n], op=ALU.add),
            r=["rt2", "rt3"], w=[dres])
        if f32dst is not None:
            off = f32dst
            POOL(lambda e: e.tensor_tensor(out=kf[:, 0, :], in0=rt[0][:, off:off + 128], in1=rt[1][:, off:off + 128],
                                           op=ALU.subtract), r=["rt0", "rt1"], w=["kf"])
            POOL(lambda e: e.tensor_tensor(out=kf[:, 1, :], in0=rt[2][:, off:off + 128], in1=rt[3][:, off:off + 128],
                                           op=ALU.add), r=["rt2", "rt3"], w=["kf"])

    WI_ALL = [("wi", k) for k in range(8)]

    def proj_chunk(bank, chunk, c0, n):
        def f(e):
            last = None
            for k in range(8):
                last = e.matmul(ps[:, bank, 0:n], lhsT=wi[:, k, chunk * 128:(chunk + 1) * 128], rhs=hT[:, k, c0:c0 + n],
                                start=(k == 0), stop=(k == 7))
            return last
        PE(f, r=WI_ALL + ["hT"], w=[("ps", bank)])

    def emit_kv_out(kdst_list, vdst_list):
        for ab in range(2):
            PE(lambda e, ab=ab: e.transpose(out=ps[:, 6 + ab, 0:128], in_=kf[:, ab, :], identity=ident_f[:]),
               r=["kf", "ident_f"], w=[("ps", 6 + ab)])
        ko = tmpA[:, 0:128].rearrange("t (g h d) -> t g h d", g=2, h=2)
        for ab in range(2):
            src = ps[:, 6 + ab, 0:128].rearrange("t (g x) -> t g x", g=2)[:, :, 0:32]
            ACT(lambda e, ab=ab, src=src: e.activation(out=ko[:, :, ab, :], in_=src, func=AF.Copy),
                r=[("ps", 6 + ab)], w=["tmpA"])
        tag = "s" if len(kdst_list) > 1 else "p"
        for (dst, src_lo, src_hi) in kdst_list:
            DMA_SP(dst, tmpA[src_lo:src_hi, 0:128], "d_ko" + tag, r=["tmpA"], w=[("o", id(dst))])
        for (dst, src_lo, src_hi) in vdst_list:
            DMA_SP(dst, vf[src_lo:src_hi, :], "d_vo" + tag, r=["vf"], w=[("o", id(dst))])

    def A1a(gi, t):
        tiles = groups[gi]
        real = [t_ for t_ in tiles if t_ != 0]
        xt = tmpA[:] if t == 0 else xall[:, t - 1, :]
        xres = "tmpA" if t == 0 else ("x", t - 1)
        blk = 0 if t == 0 else (real.index(t) + 1)
        s0 = stat_slot(1)
        ACT(lambda e, xt=xt, s0=s0: e.activation(out=junk[:], in_=xt, func=AF.Square, accum_out=stat[:, s0:s0 + 1]),
            r=[xres], w=["junk", ("st", s0)])
        rstd_op(stat[:, s0:s0 + 1], 1, D, stat[:, s0:s0 + 1], [("st", s0)], [("st", s0)])
        return (blk, xt, xres, s0)

    def A1s(h):
        blk, xt, xres, s0 = h
        DVE(lambda e, xt=xt, s0=s0: e.tensor_scalar(out=xn[:], in0=xt, scalar1=stat[:, s0:s0 + 1], scalar2=None,
                                                    op0=ALU.mult), r=[xres, ("st", s0)], w=["xn0"])

    def A1b(h):
        blk = h[0]
        transposes8(xn, "xn0", 7, hT[:, :, blk * 128:(blk + 1) * 128], "hT", gain_ap=gcol[:, 0:8].unsqueeze(2))

    def A1(gi):
        for t in groups[gi]:
            h = A1a(gi, t)
            A1s(h)
            A1b(h)

    def A3(gi, tiles, real, ntr, sample, n):
        n = ntr * 128
        if not sample:
            for g in range(4):
                w_ = 2 << g
                cur = ubuf[:, g, :]
                cres = "ubuf"
                width = 16 + n
                step = 1
                bufs = [(sA, "sA"), (sB, "sB")]
                bi = 0
                while step < w_:
                    dstt, dres_ = bufs[bi]
                    DVE(lambda e, cur=cur, dstt=dstt, step=step, width=width: e.tensor_tensor(
                        out=dstt[:, step:width], in0=cur[:, step:width], in1=cur[:, 0:width - step], op=ALU.add),
                        r=[cres], w=[dres_])
                    if step > 1:
                        pass
                    cur, cres = dstt, dres_
                    step *= 2
                    bi ^= 1
                DVE(lambda e, cur=cur, g=g, w_=w_, n=n: e.scalar_tensor_tensor(
                    out=rT[:, g, 0:n], in0=cur[:, 16:16 + n], scalar=1.0 / w_, in1=ubuf[:, g, 16:16 + n],
                    op0=ALU.mult, op1=ALU.subtract), r=[cres, "ubuf"], w=["rT"])
                if gi == 0:
                    DVE(lambda e, cur=cur, g=g: e.tensor_tensor(out=sA[:, 0:16] if cur is not sA else sB[:, 0:16],
                                                                in0=cur[:, 16:32], in1=invcnt_b[:, g, :], op=ALU.mult),
                        r=[cres, "invcnt_b"], w=["sA", "sB"])
                    DVE(lambda e, cur=cur, g=g: e.tensor_tensor(out=rT[:, g, 0:16],
                                                                in0=sA[:, 0:16] if cur is not sA else sB[:, 0:16],
                                                                in1=ubuf[:, g, 16:32], op=ALU.subtract),
                        r=["sA", "sB", "ubuf"], w=["rT"])
            if gi == 3:
                for g in range(4):
                    PE(lambda e, g=g, n=n: e.transpose(out=ps[0:15, 6, g * 128:(g + 1) * 128],
                                                       in_=ubuf[:, g, 16 + n - 15:16 + n], identity=ident_f[:]),
                       r=["ubuf", "ident_f"], w=[("ps", 6)])
                ACT(lambda e: e.activation(out=uo[0:15, :], in_=ps[0:15, 6, :], func=AF.Copy), r=[("ps", 6)], w=["tmpA"])
                DMA_SP(uout_p[:, :], uo[0:15, :], "d_uop", r=["tmpA"], w=["uout_p"])
        else:
            for g in range(4):
                POOL(lambda e, g=g: e.tensor_copy(out=us[:, g, :, 15:23],
                                                  in_=ubuf[:, g, 16:144].rearrange("c (b t) -> c b t", b=16)),
                     r=["ubuf"], w=["us"])
                w_ = 2 << g
                cur = us[:, g, :, :]
                cres = "us"
                step = 1
                bufs = [(usA, "sA"), (usB, "sB")]
                bi = 0
                while step < w_:
                    dstt, dres_ = bufs[bi]
                    POOL(lambda e, cur=cur, dstt=dstt, step=step: e.tensor_tensor(
                        out=dstt[:, :, step:23], in0=cur[:, :, step:23], in1=cur[:, :, 0:23 - step], op=ALU.add),
                        r=[cres], w=[dres_])
                    cur, cres = dstt, dres_
                    step *= 2
                    bi ^= 1
                DVE(lambda e, cur=cur, g=g, w_=w_: e.scalar_tensor_tensor(
                    out=rT[:, g, 0:128].rearrange("c (b t) -> c b t", b=16), in0=cur[:, :, 15:23], scalar=1.0 / w_,
                    in1=us[:, g, :, 15:23], op0=ALU.mult, op1=ALU.subtract), r=[cres, "us"], w=["rT"])
            for g in range(4):
                PE(lambda e, g=g: e.transpose(out=ps[:, 6, g * 128:(g + 1) * 128], in_=ubuf[:, g, 16:144],
                                              identity=ident_f[:]), r=["ubuf", "ident_f"], w=[("ps", 6)])
            ACT(lambda e: e.activation(out=uo[:], in_=ps[:, 6, :], func=AF.Copy), r=[("ps", 6)], w=["tmpA"])
            for b in range(16):
                DMA_SP(uout_s[b, 7:15, :], uo[b * 8:(b + 1) * 8, :], "d_uos", r=["tmpA"], w=[("uout_s", b)])

    A1(0)
    rp0[2]()
    for gi, tiles in enumerate(groups):
        sample = (gi == 4)
        real = [t for t in tiles if t != 0]
        ntr = len(real)
        S.mark("g%d_A1" % gi)
        c_lo = 0 if gi == 0 else 128
        ncols = (len(tiles)) * 128 if gi == 0 else ntr * 128
        tabcol = tiles[0] * 128
        if gi == 0:
            kv_segs = [(0, 128, 0), (128, 512, 128)]
        else:
            kv_segs = [(128, ntr * 128, 128)]
        pair_i = 0
        for (ca, cb_, dest, di) in PAIRS:
            if dest == "q":
                c0, n, tc0 = 128, ntr * 128, 128
            else:
                c0, n, tc0 = c_lo, ncols, tabcol
            segs = [(c0, n, tc0)] if dest == "q" else kv_segs
            for (sc0, sn, stc) in segs:
                bA = 2 * (pair_i % 2)
                pair_i += 1
                proj_chunk(bA, ca, sc0, sn)
                proj_chunk(bA + 1, cb_, sc0, sn)
                if dest == "q":
                    rope_pair(bA, bA + 1, stc, sn, qT[:, di, 0:sn], qT[:, di + 1, 0:sn], "qT")
                else:
                    f32off = None
                    if (gi == 3 and sc0 == 128) or sample:
                        f32off = (sn - 128)
                    rope_pair(bA, bA + 1, stc, sn, kT[:, 0, sc0:sc0 + sn], kT[:, 1, sc0:sc0 + sn], "kT", f32dst=f32off)
        if sample:
            for q4 in range(3):
                state_k_load(q4)
        for g in range(4):
            for (sc0, sn, _stc) in kv_segs:
                ub = 4 + g
                proj_chunk(ub, 6 + g, sc0, sn)
                if sc0 < 128:
                    ACT(lambda e, g=g, ub=ub: e.activation(out=ubuf[:, g, 0:16], in_=ps[:, ub, 112:128],
                                                           func=AF.Copy), r=[("ps", ub)], w=["ubuf"])
                else:
                    ucol = 16 + sc0 - 128
                    ACT(lambda e, g=g, sn=sn, ucol=ucol, ub=ub: e.activation(out=ubuf[:, g, ucol:ucol + sn],
                                                                             in_=ps[:, ub, 0:sn], func=AF.Copy),
                        r=[("ps", ub)], w=["ubuf"])
        A3(gi, tiles, real, ntr, sample, ntr * 128)
        for vi, t in enumerate(tiles):
            blk = 0 if t == 0 else (real.index(t) + 1)
            vb = vi % 2

            def fv(e, blk=blk, vb=vb):
                last = None
                for k in range(8):
                    last = e.matmul(ps[:, vb, 0:128], lhsT=hT[:, k, blk * 128:(blk + 1) * 128], rhs=wi[:, k, 1280:1408],
                                    start=(k == 0), stop=(k == 7))
                return last
            PE(fv, r=WI_ALL + ["hT"], w=[("ps", vb)])
            ACT(lambda e, blk=blk, vb=vb: e.activation(out=vE[:, blk, :, 0:64],
                                                       in_=ps[:, vb, 0:128].rearrange("t (g d) -> t g d", g=2),
                                                       func=AF.Copy),
                r=[("ps", vb)], w=["vE"])
            if t == 16 or t == 17:
                DVE(lambda e, vb=vb: e.tensor_copy(out=vf[:], in_=ps[:, vb, 0:128]), r=[("ps", vb)], w=["vf"])
        if gi == 3:
            emit_kv_out([(kout_p[:, :], 0, 128)], [(vout_p[:, :], 0, 128)])
        if sample:
            emit_kv_out([(kout_s[b, 120:128, :], b * 8, b * 8 + 8) for b in range(16)],
                        [(vout_s[b, 120:128, :], b * 8, b * 8 + 8) for b in range(16)])
        S.mark("g%d_A2" % gi)
        n = ntr * 128

        def S1(ti, t, hook=None):
            qcol = ti * 128
            ob0 = 2 + 2 * (ti % 2)
            issue_conv(1)
            if not sample:
                kb = []
                for j in range(2):
                    kc = (ti + j) * 128
                    blkv = ti + j
                    kb.append((lambda r, kc=kc: kT[32 * r:32 * r + 32, 0, kc:kc + 128],
                               lambda r, kc=kc: kT[32 * r:32 * r + 32, 1, kc:kc + 128],
                               lambda g, blkv=blkv: vE[:, blkv, g, :],
                               None, "kT", "vE"))
                if t == 1:
                    kb = [kb[1], kb[0]]
                    attention(t, qcol, kb, ti % 2, ob0, pe_mask=mbias[:, 1, :], hook=hook)
                else:
                    attention(t, qcol, kb, ti % 2, ob0, pe_mask=mbias[:, 0, :], hook=hook)
            else:
                build_state_kT()
                kb = []
                for b in range(16):
                    kb.append((lambda r, b=b: ksT[32 * r:32 * r + 32, 0, b, :],
                               lambda r, b=b: ksT[32 * r:32 * r + 32, 1, b, :],
                               lambda g, b=b: vsE[:, b, g, :],
                               smaskb[:, b:b + 2, :] if b % 2 == 0 else None, "hT", "vsE"))
                kb.append((lambda r: kT[32 * r:32 * r + 32, 0, 128:256],
                           lambda r: kT[32 * r:32 * r + 32, 1, 128:256],
                           lambda g: vE[:, 1, g, :], smaskb[:, 16, :], "kT", "vE"))
                def smask_bias(rd):
                    slot = rd % 2
                    DVE(lambda e, rd=rd, slot=slot: e.tensor_scalar(
                        out=mbs[:, slot, :].rearrange("k (p x) -> k p x", p=2),
                        in0=smaskb[:, 2 * rd:2 * rd + 2, :].rearrange("k j q -> k (j q)").unsqueeze(1).broadcast_to(
                            [128, 2, 256]),
                        scalar1=-1.0, scalar2=30000.0, op0=ALU.add, op1=ALU.mult),
                        r=["smaskb"], w=[("mbs", slot)])
                    return mbs[:, slot, :]
                attention(t, qcol, kb, 0, ob0, pe_mask=None, pe_mask_fn=smask_bias)

        s2slot = {}

        def S2norm(ti):
            attn_normalize(2 + 2 * (ti % 2))

        def S2pool(ti):
            qcol = ti * 128

            def fpool(e, qcol=qcol):
                last = None
                for g in range(4):
                    last = e.matmul(ps[:, 6, g * 128:(g + 1) * 128], lhsT=rT[:, g, qcol:qcol + 128], rhs=pw[:, g, :],
                                    start=(g == 0), stop=(g == 3), skip_group_check=True)
                return last
            PE(fpool, r=["rT", "pw"], w=[("ps", 6)])

        def S2statsA(ti):
            s0 = stat_slot(2)
            s2slot[ti] = s0
            ACT(lambda e, s0=s0: e.activation(out=junk[:, 0:512], in_=attn_f[:], func=AF.Square,
                                              accum_out=stat[:, s0:s0 + 1]), r=["attn_f"], w=["junk", ("st", s0)])

        def S2statsB(ti):
            s0 = s2slot[ti]
            ACT(lambda e, s0=s0: e.activation(out=junk[:, 512:1024], in_=ps[:, 6, :], func=AF.Square,
                                              accum_out=stat[:, s0 + 1:s0 + 2]), r=[("ps", 6)], w=["junk", ("st", s0)])
            rstd_op(stat[:, s0:s0 + 2], 2, 512, stat[:, s0:s0 + 2], [("st", s0)], [("st", s0)])

        def S2cat(ti):
            s0 = s2slot[ti]
            ACT(lambda e, s0=s0: e.activation(out=cat[:, 0:512], in_=attn_f[:], func=AF.Copy, scale=stat[:, s0:s0 + 1]),
                r=["attn_f", ("st", s0)], w=[("cat", 0)])
            ACT(lambda e, s0=s0: e.activation(out=cat[:, 512:1024], in_=ps[:, 6, :], func=AF.Copy,
                                              scale=stat[:, s0 + 1:s0 + 2]), r=[("ps", 6), ("st", s0)], w=[("cat", 1)])

        def S2b(ti, t):
            for h in range(2):
                bank = 7 - h
                bt = ps[:, bank, :].bitcast(BF16)

                def f(e, h=h, bt=bt):
                    last = None
                    for k in range(4):
                        kk = h * 4 + k
                        last = e.transpose(out=bt[:, k * 128:(k + 1) * 128], in_=cat[:, kk * 128:(kk + 1) * 128],
                                           identity=ident_b[:])
                    return last
                PE(f, r=[("cat", h), "ident_b"], w=[("ps", bank)])
            for h in range(2):
                bank = 7 - h
                btv = ps[:, bank, :].bitcast(BF16)[:, 0:512].rearrange("p (k t) -> p k t", k=4)
                DVE(lambda e, h=h, btv=btv: e.tensor_tensor(
                    out=catT[:, h * 4:(h + 1) * 4, :], in0=btv,
                    in1=gcol[:, 16 + h * 4:20 + h * 4].unsqueeze(2).broadcast_to([128, 4, 128]), op=ALU.mult),
                    r=[("ps", bank), "gcol"], w=[("catT", h)])

        def S3(ti, t):
            ob0 = 2 + 2 * (ti % 2)

            for h in range(2):
                def fmix(e, h=h):
                    last = None
                    for n_ in range(2):
                        for k in range(h * 4, h * 4 + 4):
                            last = e.matmul(ps[:, ob0 + n_, :], lhsT=catT[:, k, :], rhs=wo[:, k, n_ * 512:(n_ + 1) * 512],
                                            start=(k == 0), stop=(k == 7), skip_group_check=True)
                    return last
                PE(fmix, r=[("catT", h), ("wo", 0), ("wo", 4)], w=[("ps", ob0), ("ps", ob0 + 1)])
            s1 = stat_slot(1)
            mixv = ps[:, ob0:ob0 + 2, :]
            ACT(lambda e, s1=s1: e.activation(out=junk[:].rearrange("p (a b) -> p a b", a=2), in_=mixv, func=AF.Square,
                                              accum_out=stat[:, s1:s1 + 1]),
                r=[("ps", ob0), ("ps", ob0 + 1)], w=["junk", ("st", s1)])
            rstd_op(stat[:, s1:s1 + 1], 1, D, stat[:, s1:s1 + 1], [("st", s1)], [("st", s1)])
            for n_ in range(2):
                DVE(lambda e, s1=s1, n_=n_: e.scalar_tensor_tensor(
                    out=tmpA[:, n_ * 512:(n_ + 1) * 512], in0=ps[:, ob0 + n_, :], scalar=stat[:, s1:s1 + 1],
                    in1=gtab[:, n_ * 512:(n_ + 1) * 512], op0=ALU.mult, op1=ALU.mult),
                    r=[("ps", ob0 + n_), ("st", s1), "gtab"], w=["tmpA"])

        def S3add(ti, t):
            POOL(lambda e, t=t: e.tensor_tensor(out=xall[:, t - 1, :], in0=xall[:, t - 1, :], in1=tmpA[:], op=ALU.add),
                 r=["tmpA", ("x", t - 1)], w=[("x", t - 1)])

        S1(0, real[0])
        S2norm(0)
        nxt = groups[gi + 1] if gi + 1 < len(groups) else []
        pend = [None]
        rparts = rope_tables_split(nxt[0] * 128, len(nxt) * 128, 128) if nxt else []
        for ti, t in enumerate(real):
            S2pool(ti)
            S2statsA(ti)

            def mid(ti=ti):
                S2statsB(ti)
                if pend[0] is not None:
                    S3add(*pend[0])
                    pend[0] = None

            if ti + 1 < ntr:
                def hook(rd, r, ti=ti, mid=mid):
                    if rd == 0 and r == 0:
                        mid()
                    if rd == 0 and r == 3:
                        S2cat(ti)
                S1(ti + 1, real[ti + 1], hook=hook)
            else:
                mid()
                S2cat(ti)
            nh = A1a(gi + 1, nxt[ti]) if ti < len(nxt) else None
            S2b(ti, t)
            if nh is not None:
                A1s(nh)
            if ti + 1 < ntr:
                S2norm(ti + 1)
            if rparts and ti < 2:
                rparts[ti]()
            if nh is not None and ti + 1 == ntr:
                A1b(nh)
                nh = None
            S3(ti, t)
            pend[0] = (ti, t)
            if nh is not None:
                A1b(nh)
            if rparts and ti == 2:
                rparts[2]()
            if ti == 0 and gi == 0:
                load_spool()
            if ti == 0 and gi == 1:
                late_loads()
        if pend[0] is not None:
            S3add(*pend[0])
        S.mark("g%d_A4" % gi)
        if gi < 3:
            lastb = ntr
            POOL(lambda e, lastb=lastb: e.tensor_copy(out=kT[:, :, 0:128], in_=kT[:, :, lastb * 128:(lastb + 1) * 128]),
                 r=["kT"], w=["kT"])
            POOL(lambda e, lastb=lastb: e.tensor_copy(out=vE[:, 0, :, :], in_=vE[:, lastb, :, :]), r=["vE"], w=["vE"])
            POOL(lambda e, n=n: e.tensor_copy(out=sA[:, 0:64].rearrange("c (g t) -> c g t", g=4),
                                              in_=ubuf[:, :, n:n + 16]), r=["ubuf"], w=["sA"])
            POOL(lambda e: e.tensor_copy(out=ubuf[:, :, 0:16], in_=sA[:, 0:64].rearrange("c (g t) -> c g t", g=4)),
                 r=["sA"], w=["ubuf"])

    issue_conv(16)
    S.mark("phaseA")
    ENG = ["pe", "act", "dve", "pool", "sp"]
    S.barrier(ENG)
    while len(ctx) > ctxA_start:
        ctx.pop().__exit__(None, None, None)

    wg = sb("wg", [128, 8, D], BF16)
    wp = sb("wp", [128, 2, D], BF16)
    bg = sb("bg", [1, D], BF16)
    NRING = 4
    ring = [sb("ring%d" % i, [128, 8, 512], BF16) for i in range(NRING)]
    hT2 = sb("hT2", [128, 8, 512], BF16)
    hT3 = sb("hT3", [128, 8, 512], BF16)
    h1T = sb("h1T", [128, 32, 512], BF16)
    hr = [sb("hr%d" % i, [128, 512], BF16) for i in range(2)]
    ffbuf = sb("ffbuf", [128, 4, D])
    peb = sb("peb", [128, 4, 256], BF16)
    peT = sb("peT", [128, 4, 2, 128], BF16)
    xn2 = sb("xn2", [128, D], BF16)
    xnb = [xn, xn2]

    DMA_SP(gtab[:], g_post_mlp.partition_broadcast(128), "d_misc", w=["gtab"])
    wg_v = w_gate.rearrange("(k p) n -> p k n", p=128)
    wp_v = w_ple.rearrange("(k p) n -> p k n", p=128)

    bgroups = [[0, 1, 2, 3], [4, 5, 6, 7], [8, 9, 10, 11], [12, 13, 14], [15, 16]]
    blocks = []
    for gi in range(len(bgroups)):
        for b in range(8):
            blocks.append((gi, "up", b))
        for b in range(8):
            blocks.append((gi, "dn", b))
    issued = [0]

    def issue_ring(upto):
        while issued[0] < min(upto, len(blocks)):
            i = issued[0]
            gi_, kind, b = blocks[i]
            buf = ring[i % NRING]
            src = (scr_up if kind == "up" else scr_dn)[b, :, :]
            DMA_SP(buf[:].rearrange("p k n -> p (k n)"), src, "d_ring%d" % (i % NRING), r=ALL_SCR,
                   w=[("ring", i % NRING)])
            issued[0] += 1

    S.mark("pB0")
    issue_ring(NRING)
    S.mark("pB1")
    bi = 0
    xn_i = [0]

    def xnh(ti):
        return hT3[:, :, ti * 128:(ti + 1) * 128]

    def B1a(gi):
        tiles = bgroups[gi]
        for ti, t in enumerate(tiles):
            s0 = stat_slot(1)
            xb = xnh(ti)
            ACT(lambda e, t=t, s0=s0: e.activation(out=junk[:], in_=xall[:, t, :], func=AF.Square,
                                                   accum_out=stat[:, s0:s0 + 1]), r=[("x", t)], w=["junk", ("st", s0)])
            rstd_op(stat[:, s0:s0 + 1], 1, D, stat[:, s0:s0 + 1], [("st", s0)], [("st", s0)])
            DVE(lambda e, t=t, s0=s0, xb=xb: e.tensor_scalar(out=xb, in0=xall[:, t, :].rearrange("p (k c) -> p k c", k=8),
                                                             scalar1=stat[:, s0:s0 + 1], scalar2=None, op0=ALU.mult),
                r=[("x", t), ("st", s0)], w=[("hT3", ti)])

    def B1b(gi):
        tiles = bgroups[gi]
        for ti, t in enumerate(tiles):
            xb = xnh(ti)
            bank = 6 + ti % 2
            bt = ps[:, bank, :].bitcast(BF16)

            def f(e, xb=xb, bt=bt):
                last = None
                for k in range(8):
                    last = e.transpose(out=bt[:, k * 128:(k + 1) * 128], in_=xb[:, k, :], identity=ident_b[:])
                return last
            PE(f, r=[("hT3", ti), "ident_b"], w=[("ps", bank)])
            btv = bt.rearrange("p (k t) -> p k t", k=8)
            DVE(lambda e, btv=btv, ti=ti: e.tensor_tensor(out=hT2[:, :, ti * 128:(ti + 1) * 128], in0=btv,
                                                          in1=gcol[:, 8:16].unsqueeze(2).broadcast_to([128, 8, 128]),
                                                          op=ALU.mult), r=[("ps", bank), "gcol"], w=[("hT2", ti)])

    def PEload(gi):
        tiles = bgroups[gi]
        for ti, t in enumerate(tiles):
            pb_ = peb[:, ti, :]
            DMA_POOL(pb_, pein[t, :, :], "d_pe%d" % ti, w=[("peb", ti)])
        for ti, t in enumerate(tiles):
            pb_ = peb[:, ti, :]
            bank = 6 + ti % 2
            btp = ps[:, bank, :].bitcast(BF16)

            def fpet(e, btp=btp, pb_=pb_):
                last = None
                for k in range(2):
                    last = e.transpose(out=btp[:, k * 128:(k + 1) * 128], in_=pb_[:, k * 128:(k + 1) * 128],
                                       identity=ident_b[:])
                return last
            PE(fpet, r=[("peb", ti), "ident_b"], w=[("ps", bank)])
            ACT(lambda e, btp=btp, ti=ti: e.activation(out=peT[:, ti, :, :],
                                                       in_=btp[:, 0:256].rearrange("p (k t) -> p k t", k=2),
                                                       func=AF.Copy), r=[("ps", bank)], w=[("peT", ti)])

    B1a(0)
    B1b(0)
    for k in range(0, 8, 4):
        DMA_POOL(wg[:, k:k + 4, :], wg_v[:, k:k + 4, :], "d_wg", w=[("wg", k)])
    DMA_POOL(wp[:], wp_v[:, :, :], "d_wp", w=["wp"])
    DMA_POOL(bg[:], b_gate[:, :], "d_bg", w=["bg"])
    bi_box = [0]

    def B2_block(gi, b):
        tiles = bgroups[gi]
        nt_ = len(tiles)
        N = nt_ * 128
        bi = bi_box[0]
        issue_ring(bi + NRING)
        buf = ring[bi % NRING]
        bres = ("ring", bi % NRING)
        for mi in range(4):
            m = b * 4 + mi
            bank = 5 + m % 2

            def fup(e, buf=buf, mi=mi, bank=bank, N=N):
                last = None
                for k in range(8):
                    last = e.matmul(ps[:, bank, 0:N], lhsT=buf[:, k, mi * 128:(mi + 1) * 128], rhs=hT2[:, k, 0:N],
                                    start=(k == 0), stop=(k == 7))
                return last
            PE(fup, r=[bres] + [("hT2", ti) for ti in range(nt_)], w=[("ps", bank)])
            hrt = hr[m % 2]
            ACT(lambda e, bank=bank, hrt=hrt, N=N: e.activation(out=hrt[:, 0:N], in_=ps[:, bank, 0:N], func=AF.Relu),
                r=[("ps", bank)], w=[("hr", m % 2)])
            DVE(lambda e, m=m, hrt=hrt, N=N: e.tensor_tensor(out=h1T[:, m, 0:N], in0=hrt[:, 0:N], in1=hrt[:, 0:N],
                                                             op=ALU.mult), r=[("hr", m % 2)], w=[("h1T", m)])
        bi_box[0] += 1

    def B3(gi, sbase):
        tiles = bgroups[gi]
        nt_ = len(tiles)
        if gi + 1 < len(bgroups):
            B1a(gi + 1)
        for n_ in range(2):
            if n_ == 1:
                if gi + 1 < len(bgroups):
                    B1b(gi + 1)
                PEload(gi)
            for cb in range(4):
                bi = bi_box[0]
                issue_ring(bi + NRING)
                buf = ring[bi % NRING]
                bres = ("ring", bi % NRING)
                for ti in range(nt_):
                    def fdn(e, buf=buf, cb=cb, ti=ti):
                        last = None
                        for ci in range(8):
                            last = e.matmul(ps[:, ti, :], lhsT=h1T[:, cb * 8 + ci, ti * 128:(ti + 1) * 128],
                                            rhs=buf[:, ci, :], start=(cb == 0 and ci == 0), stop=(cb == 3 and ci == 7))
                        return last
                    PE(fdn, r=[bres] + [("h1T", cb * 8 + ci) for ci in range(8)], w=[("ps", ti)])
                bi_box[0] += 1
            for ti in range(nt_):
                sl = sbase + ti * 2 + n_
                ACT(lambda e, ti=ti, sl=sl: e.activation(out=junk[:, 0:512], in_=ps[:, ti, :], func=AF.Square,
                                                         accum_out=stat[:, sl:sl + 1]), r=[("ps", ti)], w=["junk", ("st", sl)])
                DVE(lambda e, ti=ti, n_=n_: e.tensor_copy(out=ffbuf[:, ti, n_ * 512:(n_ + 1) * 512], in_=ps[:, ti, :]),
                    r=[("ps", ti)], w=[("ff", ti)])

    def B4a(ti, t, sbase):
        sl = sbase + ti * 2
        POOL(lambda e, sl=sl: e.tensor_tensor(out=stat[:, sl:sl + 1], in0=stat[:, sl:sl + 1], in1=stat[:, sl + 1:sl + 2],
                                              op=ALU.add), r=[("st", sl), ("st", sl + 1)], w=[("st", sl)])
        rstd_op(stat[:, sl:sl + 1], 1, D, stat[:, sl:sl + 1], [("st", sl)], [("st", sl)])
        DVE(lambda e, ti=ti, sl=sl: e.scalar_tensor_tensor(out=ffbuf[:, ti, :], in0=ffbuf[:, ti, :],
                                                           scalar=stat[:, sl:sl + 1], in1=gtab[:], op0=ALU.mult,
                                                           op1=ALU.mult),
            r=[("ff", ti), ("st", sl), "gtab"], w=[("ff", ti)])
        DVE(lambda e, t=t, ti=ti: e.tensor_tensor(out=xall[:, t, :], in0=xall[:, t, :], in1=ffbuf[:, ti, :], op=ALU.add),
            r=[("ff", ti), ("x", t)], w=[("x", t)])
        xb = xnb[xn_i[0] % 2]
        xres = "xn%d" % (xn_i[0] % 2)
        xn_i[0] += 1
        ACT(lambda e, t=t, xb=xb: e.activation(out=xb[:], in_=xall[:, t, :], func=AF.Copy), r=[("x", t)], w=[xres])
        b4x[ti] = (xb, xres)

    b4x = {}

    def B4a_pe(ti):
        xb, xres = b4x[ti]
        transposes8(xb, xres, 7, hT3[:, :, ti * 128:(ti + 1) * 128], ("hT3", ti), gain_ap=None)

    def B4b(ti, t):
        gb = (ti % 2) * 2
        thb = ffbuf[:, ti, :]
        pbk = [4, 7]

        def fgate(e, ti=ti, gb=gb):
            last = None
            for n_ in range(2):
                for k in range(8):
                    e.matmul(ps[:, gb + n_, :], lhsT=hT3[:, k, ti * 128:(ti + 1) * 128],
                             rhs=wg[:, k, n_ * 512:(n_ + 1) * 512], start=(k == 0), stop=False)
                last = e.matmul(ps[:, gb + n_, :], lhsT=ones_row[0:1, :], rhs=bg[0:1, n_ * 512:(n_ + 1) * 512],
                                start=False, stop=True)
            return last
        PE(fgate, r=[("hT3", ti), ("wg", 0), ("wg", 4), "bg", "ones_row"], w=[("ps", gb), ("ps", gb + 1)])
        ACT(lambda e, gb=gb, thb=thb: e.activation(out=thb.rearrange("p (a b) -> p a b", a=2), in_=ps[:, gb:gb + 2, :],
                                                   func=AF.Tanh, scale=0.5),
            r=[("ps", gb), ("ps", gb + 1)], w=[("ff", ti)])
        for n_ in range(2):
            def fple(e, ti=ti, n_=n_):
                last = None
                for k in range(2):
                    last = e.matmul(ps[:, pbk[n_], :], lhsT=peT[:, ti, k, :], rhs=wp[:, k, n_ * 512:(n_ + 1) * 512],
                                    start=(k == 0), stop=(k == 1))
                return last
            PE(fple, r=[("peT", ti), "wp"], w=[("ps", pbk[n_])])
        for n_ in range(2):
            DVE(lambda e, n_=n_, thb=thb: e.scalar_tensor_tensor(out=thb[:, n_ * 512:(n_ + 1) * 512],
                                                                 in0=thb[:, n_ * 512:(n_ + 1) * 512], scalar=1.0,
                                                                 in1=ps[:, pbk[n_], :], op0=ALU.add, op1=ALU.mult),
                r=[("ff", ti), ("ps", pbk[n_])], w=[("ff", ti)])
        POOL(lambda e, t=t, thb=thb: e.tensor_tensor(out=xall[:, t, :], in0=xall[:, t, :], in1=thb, op=ALU.add),
             r=[("ff", ti), ("x", t)], w=[("x", t)])
        DMA_SP(y[t, :, :], xall[:, t, :], "d_out", r=[("x", t)], w=[("y", t)])

    for b in range(8):
        B2_block(0, b)
    DVE(lambda e: e.tensor_scalar(out=wp[:], in0=wp[:], scalar1=0.5, scalar2=None, op0=ALU.mult), r=["wp"], w=["wp"])
    for gi, tiles in enumerate(bgroups):
        nt_ = len(tiles)
        S.mark("b%d_B2" % gi)
        sbase = stat_slot(8)
        B3(gi, sbase)
        S.mark("b%d_B3" % gi)
        ups = list(range(8)) if gi + 1 < len(bgroups) else []
        order = [("el", ti) for ti in range(min(2, nt_))] + [("up", 0)]
        for ti in range(nt_):
            order.append(("pe", ti))
            if ti + 2 < nt_:
                order.append(("el", ti + 2))
            if ti % 2 == 1:
                order.append(("up", 0))
            if ti >= 2:
                order += [("b", ti - 2), ("up", 0)]
        for ti in range(max(0, nt_ - 2), nt_):
            order += [("b", ti), ("up", 0)]
        for (kind, ti) in order:
            if kind == "el":
                B4a(ti, tiles[ti], sbase)
            elif kind == "pe":
                B4a_pe(ti)
            elif kind == "b":
                B4b(ti, tiles[ti])
            elif ups:
                B2_block(gi + 1, ups.pop(0))
        while ups:
            B2_block(gi + 1, ups.pop(0))

    import os
    kstop = os.environ.get("KSTOP")
    if kstop:
        S.ops = S.ops[:S.marks[kstop]]
    if os.environ.get("KMAX"):
        S.ops = S.ops[:int(os.environ["KMAX"])]
        print("ops kept:", [(o["eng"], o["dma"]) for o in S.ops][-6:])
    semkeys = S.finalize()
    sem_cms = {}
    sems = {}
    for k in semkeys:
        cm = nc.semaphore("s_" + str(k))
        sems[k] = cm.__enter__()
        sem_cms[k] = cm
    final_waits = {}
    for op in S.ops:
        if op["dma"] is not None and op["sig"]:
            final_waits[op["semkey"]] = max(final_waits.get(op["semkey"], 0), op["sigval"])

    with nc.Block() as block:
        def emit(engname, eng):
            for op in S.ops:
                if op["eng"] != engname:
                    continue
                for (k, v) in op["waits"]:
                    eng.wait_ge(sems[k], v)
                if op["fn"] is None:
                    continue
                inst = op["fn"](eng)
                if op["sig"]:
                    inst.then_inc(sems[op["semkey"]], 16 if op["dma"] is not None else 1)
            if engname == "sp":
                for k, v in final_waits.items():
                    eng.wait_ge(sems[k], v)

        @block.tensor
        def _(e):
            emit("pe", e)

        @block.scalar
        def _(e):
            emit("act", e)

        @block.vector
        def _(e):
            emit("dve", e)

        @block.gpsimd
        def _(e):
            emit("pool", e)

        @block.sync
        def _(e):
            emit("sp", e)

    while ctx:
        ctx.pop().__exit__(None, None, None)
    psc.__exit__(None, None, None)
    for cm in sem_cms.values():
        cm.__exit__(None, None, None)
    return nc


def _win_perm():
    cols = []

    def head_of(p, r):
        return (r % 2) + 4 * (r // 2) + 2 * p
    for p in range(2):
        for half in range(2):
            for r in range(4):
                h = head_of(p, r)
                cols += list(range(h * 64 + half * 32, h * 64 + half * 32 + 32))
    for half in range(2):
        for r in range(4):
            g = r // 2
            cols += list(range(512 + g * 64 + half * 32, 512 + g * 64 + half * 32 + 32))
    cols += list(range(768, 1280))
    cols += list(range(640, 768))
    return np.array(cols, dtype=np.int64)


_NC_CACHE = {}


def kernel(x_prompt, x_sample, state_k, state_v, state_pool, p_prompt, p_sample, w_in, attn_sinks, pool_w,
           pool_scale, g_attn_out, g_pool_out, w_out, g_pre_mix, g_post_mix, g_pre_mlp, g_post_mlp, w_up, w_down,
           w_ple, w_ple_gate, b_ple_gate):
    f = lambda a: np.ascontiguousarray(np.asarray(a, dtype=np.float32))
    x_prompt, x_sample, state_k, state_v, state_pool = map(f, (x_prompt, x_sample, state_k, state_v, state_pool))
    p_prompt, p_sample = f(p_prompt), f(p_sample)
    if "nc" not in _NC_CACHE:
        _NC_CACHE["nc"] = build_nc()
    nc = _NC_CACHE["nc"]

    perm = _win_perm()
    w_in_p = f(f(w_in)[0][:, perm])

    def colv(g):
        return f(g)[0].reshape(8, 128).T
    gcols = f(np.concatenate([colv(g_pre_mix), colv(g_pre_mlp),
                              np.concatenate([f(g_attn_out)[0], f(g_pool_out)[0]]).reshape(8, 128).T], axis=1))
    kk = np.arange(128)[:, None]
    qq = np.arange(128)[None, :]
    m_prev = (kk > qq).astype(np.float32)
    m_cur = (kk <= qq).astype(np.float32)
    m_bd = ((kk // 8 == qq // 8) & (kk % 8 <= qq % 8)).astype(np.float32)
    smask = np.zeros((128, 18, 128), np.float32)
    for b in range(16):
        smask[:, b, :] = ((qq // 8 == b) & (kk > (qq % 8))).astype(np.float32)
    smask[:, 16, :] = m_bd
    inv = (np.float32(10000.0) ** (-np.arange(32, dtype=np.float32) / np.float32(32))).astype(np.float32)
    invf = f(np.tile(inv, 4).reshape(128, 1))
    poolw_l = f(np.transpose(f(pool_w)[0], (1, 0, 2)))

    in_maps = []
    for c in range(NCORES):
        xin = np.zeros((NT + 1, 128, D), np.float32)
        if c > 0:
            xin[0] = x_prompt[0, c * 2048 - 128:c * 2048]
        xin[1:17] = x_prompt[0, c * 2048:(c + 1) * 2048].reshape(16, 128, D)
        xin[17] = x_sample[c * 16:(c + 1) * 16].reshape(128, D)
        pein = np.zeros((NT, 128, 256), np.float32)
        pein[0:16] = p_prompt[0, 0, c * 2048:(c + 1) * 2048].reshape(16, 128, 256)
        pein[16] = p_sample[0, c * 16:(c + 1) * 16].reshape(128, 256)
        posv = np.zeros((NT + 1) * 128, np.float32)
        posv[0:128] = c * 2048 - 128 + np.arange(128)
        posv[128:17 * 128] = c * 2048 + np.arange(2048)
        posv[17 * 128:] = 16384 + (np.arange(128) % 8)
        if c == 0:
            posv[0:128] = 0.0
        masks = np.stack([m_prev, m_cur, m_prev if c > 0 else np.zeros_like(m_prev), m_bd], axis=1)
        invcnt = np.zeros((4, 16), np.float32)
        for g in range(4):
            w_ = 2 << g
            if c == 0:
                invcnt[g] = 1.0 / np.minimum(np.arange(16) + 1, w_)
            else:
                invcnt[g] = 1.0 / w_
        skc = state_k[0, c * 16:(c + 1) * 16]
        svc = state_v[0, c * 16:(c + 1) * 16]
        ab = np.zeros((16, 128, 2, 4, 32), np.float32)
        for half in range(2):
            for r in range(4):
                ab[:, :, half, r, :] = skc[:, :, r // 2, half * 32:(half + 1) * 32]
        sk_ab = np.transpose(ab.reshape(16, 128, 256), (1, 0, 2))
        svl = np.transpose(svc.reshape(16, 128, 128), (1, 0, 2))
        in_maps.append({
            "xin": xin, "pein": pein, "pos": posv.reshape(1, -1), "invf": invf, "w_in": w_in_p,
            "w_out": f(w_out)[0], "w_up": f(w_up)[0], "w_down": f(w_down)[0], "w_ple": f(w_ple)[0],
            "w_gate": f(w_ple_gate)[0], "b_gate": f(b_ple_gate), "pool_w": poolw_l, "pool_scale": f(pool_scale),
            "sinks": f(attn_sinks), "gcols": gcols, "g_post_mix": f(g_post_mix), "g_post_mlp": f(g_post_mlp),
            "masks": f(masks), "smask": smask, "invcnt": invcnt.reshape(1, 64), "sk_ab": f(sk_ab), "sv": f(svl),
            "sk_raw": f(skc.reshape(16, 128, 128)), "sv_raw": f(svc.reshape(16, 128, 128)),
            "spool": f(state_pool[0, c * 16:(c + 1) * 16]),
        })
    res = run_bass_kernel_spmd(nc, in_maps, core_ids=list(range(NCORES)))
    R = res.results
    y_prompt = np.concatenate([R[c]["y"][0:16].reshape(2048, D) for c in range(NCORES)], axis=0)[None]
    y_sample = np.concatenate([R[c]["y"][16].reshape(16, 8, D) for c in range(NCORES)], axis=0)
    nkp = R[7]["kout_p"].reshape(1, 1, 128, 2, 64)
    nvp = R[7]["vout_p"].reshape(1, 1, 128, 2, 64)
    nup = R[7]["uout_p"].reshape(1, 1, 15, 512)
    nks = np.concatenate([R[c]["kout_s"] for c in range(NCORES)], axis=0).reshape(1, 128, 128, 2, 64)
    nvs = np.concatenate([R[c]["vout_s"] for c in range(NCORES)], axis=0).reshape(1, 128, 128, 2, 64)
    nus = np.concatenate([R[c]["uout_s"] for c in range(NCORES)], axis=0).reshape(1, 128, 15, 512)
    out = (y_prompt, y_sample, nkp, nvp, nup, nks, nvs, nus)
    return tuple(np.ascontiguousarray(o, dtype=np.float32) for o in out)
```
